# Optimizing a Trainium2 kernel written in Bass

```python
import math
import jax, jax.numpy as jnp
from jax import lax
import numpy as np

D_MODEL = 1024
BATCH = 8
SEQ = 8192
DEPTH = 1

D_MIX = D_MODEL
D_SSM = D_MIX // 2
D_CONV = D_MIX - D_SSM
SSM_GROUP = 16
N_SSM_GROUPS = D_SSM // SSM_GROUP
SSM_STATE = 64
CONV_HEAD_DIM = 64
N_CONV_HEADS = D_CONV // CONV_HEAD_DIM
CONV_WIDTH = 3
D_IN_PROJ = D_SSM + 3 * D_CONV
D_FF = 4 * D_MODEL
RMS_EPS = 1e-6
DT_MIN = 1e-3
DT_MAX = 1e-1

kernel_name = "hymba_s5_shortconv_sandwich_block"


def rms_norm(x, g):
    xf = x.astype(jnp.float32)
    y = xf * lax.rsqrt(jnp.mean(xf * xf, axis=-1, keepdims=True) + RMS_EPS)
    return (y * g.astype(jnp.float32)).astype(x.dtype)


def _scan_combine(e1, e2):
    a1r, a1i, b1r, b1i = e1
    a2r, a2i, b2r, b2i = e2
    ar = a2r * a1r - a2i * a1i
    ai = a2r * a1i + a2i * a1r
    br = a2r * b1r - a2i * b1i + b2r
    bi = a2r * b1i + a2i * b1r + b2i
    return (ar, ai, br, bi)


def s5_group_mixer(u, lam_re, lam_im, log_dt, b_re, b_im, c_re, c_im, d_skip, w_glu):
    bsz, seq, _ = u.shape
    uf = u.astype(jnp.float32).reshape(bsz, seq, N_SSM_GROUPS, SSM_GROUP)
    lr = lam_re.astype(jnp.float32)
    li = lam_im.astype(jnp.float32)
    dt = jnp.exp(log_dt.astype(jnp.float32))[:, None]
    mag = jnp.exp(lr * dt)
    abr = mag * jnp.cos(li * dt)
    abi = mag * jnp.sin(li * dt)
    nr, ni = abr - 1.0, abi
    den = lr * lr + li * li
    coef_r = (nr * lr + ni * li) / den
    coef_i = (ni * lr - nr * li) / den
    br_ = b_re.astype(jnp.float32)
    bi_ = b_im.astype(jnp.float32)
    bbar_r = coef_r[..., None] * br_ - coef_i[..., None] * bi_
    bbar_i = coef_r[..., None] * bi_ + coef_i[..., None] * br_
    bu_r = jnp.einsum('blgh,gph->blgp', uf, bbar_r)
    bu_i = jnp.einsum('blgh,gph->blgp', uf, bbar_i)
    a_r = jnp.broadcast_to(abr, bu_r.shape)
    a_i = jnp.broadcast_to(abi, bu_i.shape)
    _, _, xr, xi = lax.associative_scan(_scan_combine, (a_r, a_i, bu_r, bu_i), axis=1)
    y = (jnp.einsum('blgp,ghp->blgh', xr, c_re.astype(jnp.float32))
         - jnp.einsum('blgp,ghp->blgh', xi, c_im.astype(jnp.float32)))
    y = y + d_skip.astype(jnp.float32) * uf
    y = jax.nn.gelu(y.reshape(bsz, seq, D_SSM))
    y = y * jax.nn.sigmoid(y @ w_glu.astype(jnp.float32))
    return y.astype(u.dtype)


def short_conv_mixer(h, b_gate, c_gate, conv_w):
    z = c_gate * h
    zp = jnp.pad(z, ((0, 0), (CONV_WIDTH - 1, 0), (0, 0)))
    conv = (conv_w[0] * zp[:, :-2] + conv_w[1] * zp[:, 1:-1] + conv_w[2] * zp[:, 2:])
    return b_gate * conv


def setup_inputs(seed: int = 0) -> dict:
    key = jax.random.key(seed)
    ks = jax.random.split(key, 24)
    f32 = jnp.float32
    L = DEPTH
    x = jax.random.normal(ks[0], (BATCH, SEQ, D_MODEL), f32)

    def gain(k, n):
        return 1.0 + 0.01 * jax.random.normal(k, (L, n), f32)

    n_idx = jnp.arange(SSM_STATE, dtype=f32)
    lam_re = -0.5 + 0.01 * jax.random.normal(ks[1], (L, N_SSM_GROUPS, SSM_STATE), f32)
    lam_im = math.pi * n_idx + 0.01 * jax.random.normal(ks[2], (L, N_SSM_GROUPS, SSM_STATE), f32)
    log_dt = jax.random.uniform(ks[3], (L, N_SSM_GROUPS), f32, math.log(DT_MIN), math.log(DT_MAX))
    b_scale = (2.0 * SSM_GROUP) ** -0.5
    c_scale = (2.0 * SSM_STATE) ** -0.5
    return {
        "x": x,
        "g_pre_mix": gain(ks[4], D_MODEL),
        "w_in": jax.random.normal(ks[5], (L, D_MODEL, D_IN_PROJ), f32) * D_MODEL ** -0.5,
        "lam_re": lam_re,
        "lam_im": lam_im,
        "log_dt": log_dt,
        "b_re": jax.random.normal(ks[6], (L, N_SSM_GROUPS, SSM_STATE, SSM_GROUP), f32) * b_scale,
        "b_im": jax.random.normal(ks[7], (L, N_SSM_GROUPS, SSM_STATE, SSM_GROUP), f32) * b_scale,
        "c_re": jax.random.normal(ks[8], (L, N_SSM_GROUPS, SSM_GROUP, SSM_STATE), f32) * c_scale,
        "c_im": jax.random.normal(ks[9], (L, N_SSM_GROUPS, SSM_GROUP, SSM_STATE), f32) * c_scale,
        "d_skip": jax.random.normal(ks[10], (L, N_SSM_GROUPS, SSM_GROUP), f32),
        "w_glu": jax.random.normal(ks[11], (L, D_SSM, D_SSM), f32) * D_SSM ** -0.5,
        "conv_w": jax.random.normal(ks[12], (L, CONV_WIDTH, D_CONV), f32) * CONV_WIDTH ** -0.5,
        "g_ssm_out": gain(ks[13], D_SSM),
        "g_conv_out": gain(ks[14], D_CONV),
        "w_out": jax.random.normal(ks[15], (L, D_MIX, D_MODEL), f32) * D_MIX ** -0.5,
        "g_post_mix": gain(ks[16], D_MODEL),
        "g_pre_mlp": gain(ks[17], D_MODEL),
        "w_up": jax.random.normal(ks[18], (L, D_MODEL, D_FF), f32) * D_MODEL ** -0.5,
        "w_down": jax.random.normal(ks[19], (L, D_FF, D_MODEL), f32) * D_FF ** -0.5,
        "g_post_mlp": gain(ks[20], D_MODEL),
    }


def reference(x, g_pre_mix, w_in, lam_re, lam_im, log_dt, b_re, b_im, c_re, c_im, d_skip,
              w_glu, conv_w, g_ssm_out, g_conv_out, w_out, g_post_mix, g_pre_mlp, w_up,
              w_down, g_post_mlp):
    for i in range(DEPTH):
        hn = rms_norm(x, g_pre_mix[i])
        proj = hn @ w_in[i]
        u_ssm = proj[..., :D_SSM]
        h_conv = proj[..., D_SSM:D_SSM + D_CONV]
        b_gate = proj[..., D_SSM + D_CONV:D_SSM + 2 * D_CONV]
        c_gate = proj[..., D_SSM + 2 * D_CONV:]
        y_ssm = s5_group_mixer(u_ssm, lam_re[i], lam_im[i], log_dt[i], b_re[i], b_im[i],
                               c_re[i], c_im[i], d_skip[i], w_glu[i])
        y_conv = short_conv_mixer(h_conv, b_gate, c_gate, conv_w[i])
        y = jnp.concatenate([rms_norm(y_ssm, g_ssm_out[i]),
                             rms_norm(y_conv, g_conv_out[i])], axis=-1)
        x = x + rms_norm(y @ w_out[i], g_post_mix[i])
        hn = rms_norm(x, g_pre_mlp[i])
        m = jnp.square(jax.nn.relu(hn @ w_up[i])) @ w_down[i]
        x = x + rms_norm(m, g_post_mlp[i])
    return x
```

```python
import math
from contextlib import ExitStack

import numpy as np
import concourse.bass as bass
import concourse.mybir as mybir
from concourse.bass_utils import run_bass_kernel_spmd

F32 = mybir.dt.float32
BF16 = mybir.dt.bfloat16
AF = mybir.ActivationFunctionType
ALU = mybir.AluOpType

D = 1024
SEQ = 8192
NB = 512
NCORES = 8
C2_INSIDE = False
S5_EVERY = 4
S5_OFF = 1
EPS = 1e-6
MAGIC = 12582912.0
TWO_PI = 2.0 * math.pi
C1 = 6.28125
C2 = TWO_PI - C1
PI_LO = 3.1415925


class Ev:
    __slots__ = ("key", "sem", "n")

    def __init__(self, key, sem, n):
        self.key, self.sem, self.n = key, sem, n


class Buf:
    def __init__(self, name):
        self.name = name
        self.last_w = None
        self.reads = []
        self.dsem = None
        self.dcnt = 0


class KB:
    ENG = ("pe", "act", "dve", "pool", "sp")

    def __init__(self, nc, stack):
        self.nc = nc
        self.stack = stack
        self.e = {"pe": nc.tensor, "act": nc.scalar, "dve": nc.vector, "pool": nc.gpsimd, "sp": nc.sync}
        self.sem = {}
        self.cnt = {}
        for k in ("pe", "act", "dve", "pool"):
            self.sem[k] = stack.enter_context(nc.semaphore("s_" + k))
            self.cnt[k] = 0
        self.waited = {k: {} for k in self.ENG}
        self.pending = {k: [] for k in self.ENG}
        self.ninstr = {k: 0 for k in self.ENG}

    def buf(self, name):
        return Buf(name)

    def bufs(self, name, n):
        return [Buf(f"{name}{i}") for i in range(n)]

    def _wait(self, e, deps):
        need = {}
        for d in deps:
            if d is None:
                continue
            if d.key == e and e == "pe":
                continue
            if d.n is None:
                raise RuntimeError(f"wait on unsignalled event of {d.key} from {e}")
            cur = need.get(d.key)
            if cur is None or d.n > cur[1]:
                need[d.key] = (d.sem, d.n)
        for key, (sem, n) in need.items():
            if self.waited[e].get(key, 0) >= n:
                continue
            self.e[e].wait_ge(sem, n)
            self.waited[e][key] = n

    @staticmethod
    def _deps(reads, writes):
        deps = []
        for b in reads:
            deps.append(b.last_w)
        for b in writes:
            deps.append(b.last_w)
            deps.extend(b.reads)
        return deps

    def op(self, e, fn, reads=(), writes=(), signal=True):
        self._wait(e, self._deps(reads, writes))
        ins = fn(self.e[e])
        self.ninstr[e] += 1
        if signal:
            self.cnt[e] += 1
            ins.then_inc(self.sem[e], 1)
            ev = Ev(e, self.sem[e], self.cnt[e])
            for p in self.pending[e]:
                p.n = self.cnt[e]
            self.pending[e] = []
        else:
            ev = Ev(e, self.sem[e], None)
            self.pending[e].append(ev)
        for b in reads:
            b.reads.append(ev)
        for b in writes:
            b.last_w = ev
            b.reads = []
        return ev

    def dma(self, q, outs_ins, reads=(), writes=(), sembuf=None):
        self._wait(q, self._deps(reads, writes))
        sb = sembuf or (writes[0] if writes else reads[0])
        if sb.dsem is None:
            sb.dsem = self.stack.enter_context(self.nc.semaphore("d_" + sb.name))
        for (o, i) in outs_ins:
            self.e[q].dma_start(out=o, in_=i).then_inc(sb.dsem, 16)
            sb.dcnt += 1
            self.ninstr[q] += 1
        ev = Ev("d_" + sb.name, sb.dsem, 16 * sb.dcnt)
        for b in reads:
            b.reads.append(ev)
        for b in writes:
            b.last_w = ev
            b.reads = []
        return ev

    def wait_all(self, e, bufs):
        deps = []
        for b in bufs:
            deps.append(b.last_w)
            deps.extend(b.reads)
        self._wait(e, deps)


_STOP = None


def _build(nblk, dbg=False):
    nc = bass.Bass("TRN2", target_bir_lowering=False)
    L = nblk * NB

    def din(name, shape, dt=F32):
        return nc.dram_tensor(name, list(shape), dt, kind="ExternalInput").ap()

    x = din("x", [L, D])
    w_in = din("w_in", [D, 2048])
    w_glu = din("w_glu", [512, 512])
    w_out = din("w_out", [D, D])
    w_up = din("w_up", [D, 4096])
    w_down = din("w_down", [4096, D])
    pA = din("pA", [128, 225])
    pB = din("pB", [128, 5, 256])
    pC = din("pC", [128, 32, 2, 16])
    pV = din("pV", [128, 64])
    gvec = din("gvec", [2, D])
    cst = din("cst", [128, 3, 128])
    out = nc.dram_tensor("out", [L, D], F32, kind="ExternalOutput").ap()
    if dbg:
        dbg_o = nc.dram_tensor("dbg", [128, 16, 512], F32, kind="ExternalOutput").ap()

    win_s = nc.dram_tensor("win_s", [128, 4, 8, 512], BF16).ap()
    wglu_s = nc.dram_tensor("wglu_s", [128, 4, 512], BF16).ap()
    wout_s = nc.dram_tensor("wout_s", [128, 2, 8, 512], BF16).ap()
    wup_s = nc.dram_tensor("wup_s", [128, 8, 8, 512], BF16).ap()
    wdn_s = nc.dram_tensor("wdn_s", [128, 2, 32, 512], BF16).ap()

    with ExitStack() as st:
        def sb(name, shape, dt):
            return st.enter_context(nc.sbuf_tensor(name, list(shape), dt))

        def ps(name, shape, dt=F32):
            return st.enter_context(nc.psum_tensor(name, list(shape), dt))

        k = KB(nc, st)

        pA_t = sb("pA_t", [128, 225], F32)
        pV_t = sb("pV_t", [128, 64], F32)
        cst_t = sb("cst_t", [128, 3, 128], F32)
        ident_bf = sb("ident_bf", [128, 128], BF16)
        swap_bf = sb("swap_bf", [128, 128], BF16)
        ones_bf = sb("ones_bf", [128, 128], BF16)
        rho = sb("rho", [128, 32], F32)
        Ec = sb("Ec", [128, 32, 128], BF16)
        Es = sb("Es", [128, 32, 128], BF16)
        Bp = sb("Bp", [128, 2, 4, 4, 2, 128], BF16)
        Cp = sb("Cp", [128, 2, 32, 2, 64], BF16)
        Dm = sb("Dm", [128, 2, 4, 128], BF16)
        gpm_bc = sb("gpm_bc", [128, D], F32)
        gpo_bc = sb("gpo_bc", [128, D], F32)
        NR = 3
        ring = [sb(f"ring{i}", [128, 8, 512], BF16) for i in range(NR)]
        xs = [sb(f"xs{i}", [128, 4, D], F32) for i in range(2)]
        xT = sb("xT", [128, 8, NB], BF16)
        xbn2 = sb("xbn2", [128, 2, D], BF16)
        xbn = [xbn2[:, 0, :], xbn2[:, 1, :]]
        stat = sb("stat", [128, 80], F32)
        big = sb("big", [128, 32, NB], BF16)
        bigf = big[:].bitcast(F32)
        yn = sb("yn", [128, 8, NB], BF16)
        sq2 = sb("sq2", [128, 2, NB], BF16)
        sq = [sq2[:, 0, :], sq2[:, 1, :]]
        junk = sq2[:].rearrange("p a b -> p (a b)")
        rcs0 = sb("rcs0", [128, NB], F32)
        rcs = [rcs0, rcs0]
        rcs1p = sb("rcs1p", [128, 128], F32)
        s5w = sb("s5w", [128, 6, 512], F32)
        Vt = [s5w[:, i, :].rearrange("p (a b) -> p a b", a=4) for i in range(2)]
        tmpm1 = s5w[:, 2, :].rearrange("p (a b) -> p a b", a=4)
        tmpm = [tmpm1, tmpm1]
        Wt = [s5w[:, 3 + i, :].rearrange("p (a b) -> p a b", a=4) for i in range(2)]
        tmpE = [s5w[:, 5, :], None]
        s5wf = s5w[:].rearrange("p a b -> p (a b)")
        Bvv = s5wf[:, 0:2048].rearrange("p (s t v c) -> p s t v c", s=2, t=4, v=2)
        Cv = s5wf[:, 2048:3072].rearrange("p (g v h) -> p g v h", v=2, h=16)
        Q1b = [sb(f"Q1b{i}", [128, 4, 130], BF16) for i in range(2)]
        Q2b = [sb(f"Q2b{i}", [128, 4, 130], BF16) for i in range(2)]
        cb = sb("cb", [128, 32], F32)
        wl = sb("wl", [128, 32], F32)
        cq = sb("cq", [128, 6, 8], F32)
        cqb = sb("cqb", [128, 4, 8], BF16)
        yg32 = sb("yg32", [128, 4, NB], F32)
        ygb = xbn2[:].rearrange("p a (b c) -> p (a b) c", b=2)
        tmpE[1] = sb("tmpE1", [128, NB], F32)
        sg = tmpE
        zhalo = sb("zhalo", [128, 4, 2], F32)
        rl = tmpE

        bigflat = big[:].rearrange("p a b -> p (a b)")
        bigfflat = bigflat.bitcast(F32)
        ub_t = sb("ub_t", [128, 4, NB], BF16)
        ub = ub_t[:]
        th = bigfflat[:, 1024:3072].rearrange("p (a b) -> p a b", a=4)
        zb = bigfflat[:, 3072:5128].rearrange("p (a b) -> p a b", a=4)
        bg = bigfflat[:, 5128:7176].rearrange("p (a b) -> p a b", a=4)
        tcv = bigfflat[:, 7176:7688]
        yc = th
        h1T = big

        PS = [ps(f"ps{i}", [128, NB], F32) for i in range(8)]
        PT = [PS[7][:].bitcast(BF16).rearrange("p (a b) -> p a b", a=8),
              PS[0][:].bitcast(BF16).rearrange("p (a b) -> p a b", a=8)]

        b_const = k.buf("const")
        b_ring = k.bufs("ring", NR)
        b_xs = [k.bufs(f"xs{i}_", 4) for i in range(2)]
        b_xT = k.buf("xT")
        b_xbn = k.bufs("xbn", 2)
        b_junk = k.buf("junk")
        b_stat = k.bufs("stat", 8)
        b_statD = k.bufs("statD", 4)
        b_statE = k.bufs("statE", 4)
        b_big = k.buf("big")
        b_ub = k.buf("ub")
        b_th = k.bufs("th", 4)
        b_z = k.bufs("z", 4)
        b_bg = k.bufs("bg", 4)
        b_tcv = k.buf("tcv")
        b_yc = b_th
        b_h1 = k.bufs("h1T", 32)
        b_yn = k.bufs("yn", 8)
        b_sq = k.bufs("sq", 2)
        b_rcs0 = k.buf("rcs")
        b_rcs = [b_rcs0, b_rcs0]
        b_Vt = k.bufs("Vt", 2)
        b_tmpm0 = k.buf("tmpm")
        b_tmpm = [b_tmpm0, b_tmpm0]
        b_Wt = k.bufs("Wt", 2)
        b_Q = k.bufs("Q", 2)
        b_cb = k.bufs("cb", 4)
        b_wl = k.bufs("wl", 4)
        b_cq = k.buf("cq")
        b_cqb = k.buf("cqb")
        b_yg = k.bufs("yg", 4)
        b_ygb = k.buf("ygb")
        b_tmpE = k.bufs("tmpE", 2)
        b_sg = b_tmpE
        b_zh = k.buf("zhalo")
        b_rl = b_tmpE
        b_PS = k.bufs("PS", 8)
        b_PT = [b_PS[7], b_PS[0]]
        b_out = k.buf("outdram")
        b_scr = {n: k.buf("scr_" + n) for n in ("win", "wglu", "wout", "wup", "wdn")}
        b_dbg = k.buf("dbg")
        MIX_ALL = b_th + b_z + b_bg + [b_tcv]

        k.dma("sp", [(pA_t[:], pA[:, :])], writes=[b_const])
        k.dma("sp", [(pV_t[:], pV[:, :])], writes=[b_const])
        k.dma("sp", [(cst_t[:], cst[:, :, :])], writes=[b_const])
        k.dma("sp", [(gpm_bc[:], gvec[0:1, :].partition_broadcast(128))], writes=[b_const])
        k.dma("sp", [(gpo_bc[:], gvec[1:2, :].partition_broadcast(128))], writes=[b_const])
        pB_t = xs[1][:].rearrange("p a b -> p (a b)")[:, 0:1280].rearrange("p (a b) -> p a b", a=5)
        pC_t = xs[1][:].rearrange("p a b -> p (a b)")[:, 1280:2304].rearrange("p (a b c) -> p a b c", a=32, b=2)
        b_pro = k.buf("pro")
        k.dma("sp", [(pB_t, pB[:, :, :])], writes=[b_pro])
        k.dma("sp", [(pC_t, pC[:, :, :, :])], writes=[b_pro])

        def V(fn, reads=(), writes=()):
            return k.op("dve", fn, reads=reads, writes=writes)

        def A(fn, reads=(), writes=()):
            return k.op("act", fn, reads=reads, writes=writes)

        def G(fn, reads=(), writes=()):
            return k.op("pool", fn, reads=reads, writes=writes)

        C_ = [b_const]
        P_ = [b_pro]
        CP = [b_const, b_pro]

        V(lambda e: e.tensor_copy(out=ident_bf[:], in_=cst_t[:, 0, :]), reads=C_, writes=C_)
        V(lambda e: e.tensor_copy(out=swap_bf[:], in_=cst_t[:, 1, :]), reads=C_, writes=C_)
        V(lambda e: e.memset(ones_bf[:], 1.0), writes=C_)
        V(lambda e: e.memset(cb[:], 0.0), writes=b_cb)
        V(lambda e: e.memset(zhalo[:], 0.0), writes=[b_zh])
        for i_ in range(2):
            V(lambda e, i_=i_: e.memset(Q2b[i_][:], 0.0), writes=[b_Q[i_]])
            V(lambda e, i_=i_: e.memset(Q1b[i_][:], 0.0), writes=[b_Q[i_]])

        lrA = pA_t[:, 0:32]
        liA = pA_t[:, 32:64]
        ldA = pA_t[:, 64:96]
        sgnA = pA_t[:, 96:97]
        kvec = pA_t[:, 97:225]
        mask4 = pV_t[:, 40:44]

        xs0f = xs[0][:].rearrange("p a b -> p (a b)")
        tB = [xs0f[:, 256 * i:256 * (i + 1)] for i in range(16)]
        xs1f = xs[1][:].rearrange("p a b -> p (a b)")
        tAs = [xs1f[:, 2304 + 32 * i: 2304 + 32 * (i + 1)] for i in range(16)]
        Cav = xs1f[:, 2816:3840].rearrange("p (g v h) -> p g v h", v=2, h=16)

        def range_reduce(src, nbuf, dst):
            V(lambda e: e.tensor_scalar(out=nbuf, in0=src, scalar1=1.0 / TWO_PI, scalar2=MAGIC, op0=ALU.mult, op1=ALU.add), reads=CP, writes=P_)
            V(lambda e: e.tensor_scalar(out=nbuf, in0=nbuf, scalar1=-MAGIC, scalar2=None, op0=ALU.add), reads=CP, writes=P_)
            V(lambda e: e.scalar_tensor_tensor(out=dst, in0=nbuf, scalar=-C1, in1=src, op0=ALU.mult, op1=ALU.add), reads=CP, writes=P_)
            V(lambda e: e.scalar_tensor_tensor(out=dst, in0=nbuf, scalar=-C2, in1=dst, op0=ALU.mult, op1=ALU.add), reads=CP, writes=P_)
            V(lambda e: e.tensor_scalar(out=dst, in0=dst, scalar1=PI_LO, scalar2=-PI_LO, op0=ALU.min, op1=ALU.max), reads=CP, writes=P_)

        def shift_quarter(src, mbuf, dst):
            V(lambda e: e.tensor_scalar(out=dst, in0=src, scalar1=math.pi / 2, scalar2=None, op0=ALU.add), reads=CP, writes=P_)
            V(lambda e: e.tensor_scalar(out=mbuf, in0=dst, scalar1=math.pi, scalar2=-TWO_PI, op0=ALU.is_gt, op1=ALU.mult), reads=CP, writes=P_)
            V(lambda e: e.tensor_tensor(out=dst, in0=dst, in1=mbuf, op=ALU.add), reads=CP, writes=P_)
            V(lambda e: e.tensor_scalar(out=dst, in0=dst, scalar1=PI_LO, scalar2=-PI_LO, op0=ALU.min, op1=ALU.max), reads=CP, writes=P_)

        dtA, lrdtA, phiA, nA = tAs[0], tAs[1], tAs[2], tAs[3]
        A(lambda e: e.activation(out=dtA, in_=ldA, func=AF.Exp), reads=CP, writes=P_)
        V(lambda e: e.tensor_tensor(out=lrdtA, in0=lrA, in1=dtA, op=ALU.mult), reads=CP, writes=P_)
        rho1, sinA, cosA, tqA, mqA, arA, aiA, phi2 = tAs[4], tAs[5], tAs[6], tAs[7], tAs[8], tAs[9], tAs[10], tAs[11]
        A(lambda e: e.activation(out=rho1, in_=lrdtA, func=AF.Exp), reads=CP, writes=P_)
        V(lambda e: e.tensor_tensor(out=rho[:], in0=rho1, in1=rho1, op=ALU.mult), reads=CP, writes=C_)
        V(lambda e: e.tensor_tensor(out=phiA, in0=liA, in1=dtA, op=ALU.mult), reads=CP, writes=P_)
        range_reduce(phiA, nA, phiA)
        A(lambda e: e.activation(out=sinA, in_=phiA, func=AF.Sin), reads=CP, writes=P_)
        shift_quarter(phiA, mqA, tqA)
        A(lambda e: e.activation(out=cosA, in_=tqA, func=AF.Sin), reads=CP, writes=P_)
        V(lambda e: e.tensor_tensor(out=arA, in0=rho1, in1=cosA, op=ALU.mult), reads=CP, writes=P_)
        V(lambda e: e.tensor_tensor(out=aiA, in0=rho1, in1=sinA, op=ALU.mult), reads=CP, writes=P_)
        V(lambda e: e.tensor_scalar(out=phi2, in0=phiA, scalar1=2.0, scalar2=None, op0=ALU.mult), reads=CP, writes=P_)
        range_reduce(phi2, nA, phi2)
        phiA = phi2
        big1 = bigfflat[:, 0:4096].rearrange("p (g k) -> p g k", g=32)
        big2 = bigfflat[:, 4096:8192].rearrange("p (g k) -> p g k", g=32)
        b_bigp = [b_big, b_pro, b_const]
        V(lambda e: e.tensor_tensor(out=big1, in0=phiA.unsqueeze(2).broadcast_to([128, 32, 128]),
                                    in1=kvec.unsqueeze(1).broadcast_to([128, 32, 128]), op=ALU.mult),
          reads=CP, writes=[b_big])
        b1f = bigfflat[:, 0:4096]
        b2f = bigfflat[:, 4096:8192]

        def big_reduce():
            V(lambda e: e.tensor_scalar(out=b2f, in0=b1f, scalar1=1.0 / TWO_PI, scalar2=MAGIC, op0=ALU.mult, op1=ALU.add), reads=b_bigp, writes=[b_big])
            V(lambda e: e.tensor_scalar(out=b2f, in0=b2f, scalar1=-MAGIC, scalar2=None, op0=ALU.add), reads=b_bigp, writes=[b_big])
            V(lambda e: e.scalar_tensor_tensor(out=b1f, in0=b2f, scalar=-C1, in1=b1f, op0=ALU.mult, op1=ALU.add), reads=b_bigp, writes=[b_big])
            V(lambda e: e.scalar_tensor_tensor(out=b1f, in0=b2f, scalar=-C2, in1=b1f, op0=ALU.mult, op1=ALU.add), reads=b_bigp, writes=[b_big])
            V(lambda e: e.tensor_scalar(out=b1f, in0=b1f, scalar1=PI_LO, scalar2=-PI_LO, op0=ALU.min, op1=ALU.max), reads=b_bigp, writes=[b_big])

        big_reduce()
        A(lambda e: e.activation(out=Es[:].rearrange("p g k -> p (g k)"), in_=b1f, func=AF.Sin), reads=b_bigp, writes=C_)
        V(lambda e: e.tensor_scalar(out=b1f, in0=b1f, scalar1=math.pi / 2, scalar2=None, op0=ALU.add), reads=b_bigp, writes=[b_big])
        V(lambda e: e.tensor_scalar(out=b2f, in0=b1f, scalar1=math.pi, scalar2=-TWO_PI, op0=ALU.is_gt, op1=ALU.mult), reads=b_bigp, writes=[b_big])
        V(lambda e: e.tensor_tensor(out=b1f, in0=b1f, in1=b2f, op=ALU.add), reads=b_bigp, writes=[b_big])
        V(lambda e: e.tensor_scalar(out=b1f, in0=b1f, scalar1=PI_LO, scalar2=-PI_LO, op0=ALU.min, op1=ALU.max), reads=b_bigp, writes=[b_big])
        A(lambda e: e.activation(out=Ec[:].rearrange("p g k -> p (g k)"), in_=b1f, func=AF.Sin), reads=b_bigp, writes=C_)

        def pBv(i):
            return pB_t[:, i, :]
        lrB, liB, ldB, brB, biB = (pBv(i) for i in range(5))
        dtB, lrdtB, magB, phiB, nB_, sinB, cosB, mB = tB[0], tB[1], tB[2], tB[3], tB[4], tB[5], tB[6], tB[7]
        A(lambda e: e.activation(out=dtB, in_=ldB, func=AF.Exp), reads=CP, writes=P_)
        V(lambda e: e.tensor_tensor(out=lrdtB, in0=lrB, in1=dtB, op=ALU.mult), reads=CP, writes=P_)
        A(lambda e: e.activation(out=magB, in_=lrdtB, func=AF.Exp), reads=CP, writes=P_)
        V(lambda e: e.tensor_tensor(out=phiB, in0=liB, in1=dtB, op=ALU.mult), reads=CP, writes=P_)
        range_reduce(phiB, nB_, phiB)
        A(lambda e: e.activation(out=sinB, in_=phiB, func=AF.Sin), reads=CP, writes=P_)
        shift_quarter(phiB, mB, phiB)
        A(lambda e: e.activation(out=cosB, in_=phiB, func=AF.Sin), reads=CP, writes=P_)
        abr, abi, den, t1, t2, crB, ciB = tB[0], tB[1], tB[3], tB[4], tB[7], tB[8], tB[9]
        V(lambda e: e.tensor_tensor(out=abr, in0=magB, in1=cosB, op=ALU.mult), reads=CP, writes=P_)
        V(lambda e: e.tensor_tensor(out=abi, in0=magB, in1=sinB, op=ALU.mult), reads=CP, writes=P_)
        nrB = tB[12]
        V(lambda e: e.tensor_scalar(out=nrB, in0=abr, scalar1=-1.0, scalar2=None, op0=ALU.add), reads=CP, writes=P_)
        V(lambda e: e.tensor_tensor(out=den, in0=lrB, in1=lrB, op=ALU.mult), reads=CP, writes=P_)
        V(lambda e: e.tensor_tensor(out=t1, in0=liB, in1=liB, op=ALU.mult), reads=CP, writes=P_)
        V(lambda e: e.tensor_tensor(out=den, in0=den, in1=t1, op=ALU.add), reads=CP, writes=P_)
        V(lambda e: e.reciprocal(out=den, in_=den), reads=CP, writes=P_)
        V(lambda e: e.tensor_tensor(out=t1, in0=nrB, in1=lrB, op=ALU.mult), reads=CP, writes=P_)
        V(lambda e: e.tensor_tensor(out=t2, in0=abi, in1=liB, op=ALU.mult), reads=CP, writes=P_)
        V(lambda e: e.tensor_tensor(out=t1, in0=t1, in1=t2, op=ALU.add), reads=CP, writes=P_)
        V(lambda e: e.tensor_tensor(out=crB, in0=t1, in1=den, op=ALU.mult), reads=CP, writes=P_)
        V(lambda e: e.tensor_tensor(out=t1, in0=abi, in1=lrB, op=ALU.mult), reads=CP, writes=P_)
        V(lambda e: e.tensor_tensor(out=t2, in0=nrB, in1=liB, op=ALU.mult), reads=CP, writes=P_)
        V(lambda e: e.tensor_tensor(out=t1, in0=t1, in1=t2, op=ALU.subtract), reads=CP, writes=P_)
        V(lambda e: e.tensor_tensor(out=ciB, in0=t1, in1=den, op=ALU.mult), reads=CP, writes=P_)
        bbr = tB[10]
        bbi = tB[11]
        V(lambda e: e.tensor_tensor(out=t1, in0=crB, in1=brB, op=ALU.mult), reads=CP, writes=P_)
        V(lambda e: e.tensor_tensor(out=t2, in0=ciB, in1=biB, op=ALU.mult), reads=CP, writes=P_)
        V(lambda e: e.tensor_tensor(out=bbr, in0=t1, in1=t2, op=ALU.subtract), reads=CP, writes=P_)
        V(lambda e: e.tensor_tensor(out=t1, in0=crB, in1=biB, op=ALU.mult), reads=CP, writes=P_)
        V(lambda e: e.tensor_tensor(out=t2, in0=ciB, in1=brB, op=ALU.mult), reads=CP, writes=P_)
        V(lambda e: e.tensor_tensor(out=bbi, in0=t1, in1=t2, op=ALU.add), reads=CP, writes=P_)
        b0r = tB[13]
        b0i = tB[14]
        V(lambda e: e.tensor_tensor(out=t1, in0=abr, in1=bbr, op=ALU.mult), reads=CP, writes=P_)
        V(lambda e: e.tensor_tensor(out=t2, in0=abi, in1=bbi, op=ALU.mult), reads=CP, writes=P_)
        V(lambda e: e.tensor_tensor(out=b0r, in0=t1, in1=t2, op=ALU.subtract), reads=CP, writes=P_)
        V(lambda e: e.tensor_tensor(out=t1, in0=abr, in1=bbi, op=ALU.mult), reads=CP, writes=P_)
        V(lambda e: e.tensor_tensor(out=t2, in0=abi, in1=bbr, op=ALU.mult), reads=CP, writes=P_)
        V(lambda e: e.tensor_tensor(out=b0i, in0=t1, in1=t2, op=ALU.add), reads=CP, writes=P_)
        for sl, (xr, xi) in enumerate(((b0r, b0i), (bbr, bbi))):
            xr3 = xr.rearrange("p (t q) -> p t q", t=4)
            xi3 = xi.rearrange("p (t q) -> p t q", t=4)
            V(lambda e, sl=sl, xr3=xr3: e.tensor_copy(out=Bvv[:, sl, :, 0, 0:64], in_=xr3), reads=CP, writes=P_)
            V(lambda e, sl=sl, xi3=xi3: e.tensor_copy(out=Bvv[:, sl, :, 0, 64:128], in_=xi3), reads=CP, writes=P_)
            V(lambda e, sl=sl, xi3=xi3: e.tensor_copy(out=Bvv[:, sl, :, 1, 0:64], in_=xi3), reads=CP, writes=P_)
            V(lambda e, sl=sl, xr3=xr3: e.tensor_scalar(out=Bvv[:, sl, :, 1, 64:128], in0=xr3, scalar1=-1.0, scalar2=None, op0=ALU.mult), reads=CP, writes=P_)
            for gq in range(4):
                V(lambda e, gq=gq, sl=sl: e.tensor_scalar(out=Bp[:, sl, :, gq, :, :], in0=Bvv[:, sl], scalar1=mask4[:, gq:gq + 1], scalar2=None, op0=ALU.mult),
                  reads=CP, writes=C_)

        V(lambda e: e.tensor_scalar(out=Cv[:, :, 0, :], in0=pC_t[:, :, 0, :], scalar1=sgnA, scalar2=None, op0=ALU.mult), reads=CP, writes=P_)
        V(lambda e: e.tensor_scalar(out=Cv[:, :, 1, :], in0=pC_t[:, :, 1, :], scalar1=-1.0, scalar2=None, op0=ALU.mult), reads=CP, writes=P_)
        arb = arA.unsqueeze(2).broadcast_to([128, 32, 16])
        aib = aiA.unsqueeze(2).broadcast_to([128, 32, 16])
        ct1 = rcs0[:].rearrange("p (g h) -> p g h", h=16)
        ct2 = tmpE[1][:].rearrange("p (g h) -> p g h", h=16)
        V(lambda e: e.tensor_tensor(out=ct1, in0=Cv[:, :, 0, :], in1=arb, op=ALU.mult), reads=CP, writes=P_)
        V(lambda e: e.tensor_tensor(out=ct2, in0=Cv[:, :, 1, :], in1=aib, op=ALU.mult), reads=CP, writes=P_)
        V(lambda e: e.tensor_tensor(out=Cav[:, :, 0, :], in0=ct1, in1=ct2, op=ALU.add), reads=CP, writes=P_)
        V(lambda e: e.tensor_tensor(out=ct1, in0=Cv[:, :, 1, :], in1=arb, op=ALU.mult), reads=CP, writes=P_)
        V(lambda e: e.tensor_tensor(out=ct2, in0=Cv[:, :, 0, :], in1=aib, op=ALU.mult), reads=CP, writes=P_)
        V(lambda e: e.tensor_tensor(out=Cav[:, :, 1, :], in0=ct1, in1=ct2, op=ALU.subtract), reads=CP, writes=P_)
        V(lambda e: e.memset(Cp[:], 0.0), writes=C_)
        for sl, cvx in enumerate((Cav, Cv)):
            Cp5 = Cp[:, sl].rearrange("p (a b) v c -> p a b v c", b=4)
            Cv5 = cvx.rearrange("p (a b) v h -> p a b v h", b=4)
            for gq in range(4):
                V(lambda e, gq=gq, Cp5=Cp5, Cv5=Cv5: e.tensor_copy(out=Cp5[:, :, gq, :, gq * 16:(gq + 1) * 16], in_=Cv5[:, :, gq, :, :]), reads=CP, writes=C_)
        ynf = yn[:].rearrange("p a b -> p (a b)")
        Bvh = ynf[:, 0:512].rearrange("p (t c) -> p t c", t=4)
        Bvl = ynf[:, 512:1024].rearrange("p (t c) -> p t c", t=4)
        Bsth = ynf[:, 1024:1536].rearrange("p (t c) -> p t c", t=4)
        Bstl = ynf[:, 1536:2048].rearrange("p (t c) -> p t c", t=4)
        Csth = ynf[:, 2048:2560]
        Cstl = ynf[:, 2560:3072]
        dif = rcs[0][:]
        V(lambda e: e.tensor_copy(out=Bvh, in_=Bvv[:, 1, :, 0, :]), reads=CP, writes=P_)
        V(lambda e: e.tensor_tensor(out=dif.rearrange("p (t c) -> p t c", t=4), in0=Bvv[:, 1, :, 0, :], in1=Bvh, op=ALU.subtract), reads=CP, writes=P_)
        V(lambda e: e.tensor_copy(out=Bvl, in_=dif.rearrange("p (t c) -> p t c", t=4)), reads=CP, writes=P_)
        cv0 = Cv[:, :, 0, :]
        V(lambda e: e.tensor_copy(out=Csth.rearrange("p (g h) -> p g h", h=16), in_=cv0), reads=CP, writes=P_)
        V(lambda e: e.tensor_tensor(out=dif.rearrange("p (g h) -> p g h", h=16), in0=cv0, in1=Csth.rearrange("p (g h) -> p g h", h=16), op=ALU.subtract), reads=CP, writes=P_)
        V(lambda e: e.tensor_copy(out=Cstl, in_=dif), reads=CP, writes=P_)
        for t in range(4):
            k.op("pe", lambda e, t=t: e.transpose(PT[0][:, t, :], Bvh[:, t, :], ident_bf[:]), reads=CP, writes=[b_PT[0]], signal=False)
            k.op("pe", lambda e, t=t: e.transpose(PT[0][:, 4 + t, :], Bvl[:, t, :], ident_bf[:]), reads=CP, writes=[b_PT[0]], signal=(t == 3))
        A(lambda e: e.activation(out=Bsth, in_=PT[0][:, 0:4, :], func=AF.Copy), reads=[b_PT[0]], writes=P_)
        A(lambda e: e.activation(out=Bstl, in_=PT[0][:, 4:8, :], func=AF.Copy), reads=[b_PT[0]], writes=P_)
        for t in range(4):
            cs = slice(t * 128, (t + 1) * 128)
            k.op("pe", lambda e, t=t, cs=cs: e.matmul(PS[0][:, cs], lhsT=Bsth[:, t, :], rhs=Csth[:, cs], start=True, stop=False), reads=CP, writes=[b_PS[0]], signal=False)
            k.op("pe", lambda e, t=t, cs=cs: e.matmul(PS[0][:, cs], lhsT=Bsth[:, t, :], rhs=Cstl[:, cs], start=False, stop=False), reads=CP, writes=[b_PS[0]], signal=False)
            k.op("pe", lambda e, t=t, cs=cs: e.matmul(PS[0][:, cs], lhsT=Bstl[:, t, :], rhs=Csth[:, cs], start=False, stop=True), reads=CP, writes=[b_PS[0]], signal=(t == 3))
        for t in range(4):
            cs = slice(t * 128, (t + 1) * 128)
            V(lambda e, t=t: e.tensor_scalar(out=Dm[:, 1, t, :], in0=cst_t[:, 0, :], scalar1=pV_t[:, t:t + 1], scalar2=None, op0=ALU.mult), reads=CP, writes=C_)
            V(lambda e, t=t, cs=cs: e.tensor_tensor(out=rcs1p[:], in0=PS[0][:, cs], in1=cst_t[:, 2, :], op=ALU.mult), reads=[b_PS[0], b_const, b_pro], writes=P_)
            V(lambda e, t=t: e.scalar_tensor_tensor(out=Dm[:, 0, t, :], in0=cst_t[:, 0, :], scalar=pV_t[:, t:t + 1], in1=rcs1p[:], op0=ALU.mult, op1=ALU.add),
              reads=CP, writes=C_)
        for en in ("act", "dve", "pe", "pool"):
            k.wait_all(en, [b_pro, b_const, b_big])

        stg_f = [yg32[:, 0:2, :].rearrange("p a b -> p (a b)"), yg32[:, 2:4, :].rearrange("p a b -> p (a b)")]
        stg_b = [yn[:, 0:2, :].rearrange("p a b -> p (a b)"), yn[:, 2:4, :].rearrange("p a b -> p (a b)")]
        b_stgf = k.bufs("stgf", 2)
        b_stgb = k.bufs("stgb", 2)
        one_col = pA_t[:, 97:98]
        chunks = []
        for kk in range(8):
            for c0 in range(0, 2048, 1024):
                chunks.append((w_in[kk * 128:(kk + 1) * 128, c0:c0 + 1024], pV_t[:, 16 + kk:17 + kk], win_s[:, c0 // 512:c0 // 512 + 2, kk, :], 1024))
        for kk in range(4):
            chunks.append((w_glu[kk * 128:(kk + 1) * 128, :], one_col, wglu_s[:, kk, :], 512))
        for kk in range(8):
            chunks.append((w_out[kk * 128:(kk + 1) * 128, :], pV_t[:, 32 + kk:33 + kk], wout_s[:, :, kk, :], 1024))
        late_chunks = []
        for kk in range(8):
            for c0 in range(0, 4096, 1024):
                late_chunks.append((w_up[kk * 128:(kk + 1) * 128, c0:c0 + 1024], pV_t[:, 24 + kk:25 + kk], wup_s[:, c0 // 512:c0 // 512 + 2, kk, :], 1024))
        for kk in range(32):
            late_chunks.append((w_down[kk * 128:(kk + 1) * 128, :], one_col, wdn_s[:, :, kk, :], 1024))
        NLS = 4
        lstg_f = [bigfflat[:, 1024 * i:1024 * (i + 1)] for i in range(NLS)]
        lstg_b = [bigflat[:, 8192 + 1024 * i: 8192 + 1024 * (i + 1)] for i in range(NLS)]
        b_lstgf = k.bufs("lstgf", NLS)
        b_lstgb = k.bufs("lstgb", NLS)
        late_state = [0]

        def late_load(ci, first=False):
            src, g, dst, cw = late_chunks[ci]
            i = ci % NLS
            k.dma("sp", [(lstg_f[i][:, 0:cw], src)], writes=[b_lstgf[i]] + (MIX_ALL if first else []))

        def late_convert(n):
            for _ in range(n):
                ci = late_state[0]
                if ci >= len(late_chunks):
                    return
                if ci == 0:
                    for c_ in range(NLS - 1):
                        late_load(c_, first=(c_ == 0))
                if ci + NLS - 1 < len(late_chunks):
                    late_load(ci + NLS - 1)
                src, gcol, dst, cw = late_chunks[ci]
                i = ci % NLS
                A(lambda e, i=i, cw=cw, gcol=gcol: e.activation(out=lstg_b[i][:, 0:cw], in_=lstg_f[i][:, 0:cw], func=AF.Copy, scale=gcol),
                  reads=[b_lstgf[i], b_const], writes=[b_lstgb[i]])
                k.dma("sp", [(dst, lstg_b[i][:, 0:cw].rearrange("p (a b) -> p a b", b=512))], reads=[b_lstgb[i]], sembuf=b_lstgb[i])
                late_state[0] += 1

        def cv_load(ci):
            src, g, dst, cw = chunks[ci]
            i = ci % 2
            k.dma("sp", [(stg_f[i][:, 0:cw], src)], writes=[b_stgf[i]])

        cv_load(0)
        for ci in range(len(chunks)):
            src, gcol, dst, cw = chunks[ci]
            i = ci % 2
            if ci + 1 < len(chunks):
                cv_load(ci + 1)
            if i == 0:
                A(lambda e, i=i, cw=cw, gcol=gcol: e.activation(out=stg_b[i][:, 0:cw], in_=stg_f[i][:, 0:cw], func=AF.Copy, scale=gcol),
                  reads=[b_stgf[i], b_const], writes=[b_stgb[i]])
            else:
                V(lambda e, i=i, cw=cw, gcol=gcol: e.tensor_scalar(out=stg_b[i][:, 0:cw], in0=stg_f[i][:, 0:cw], scalar1=gcol, scalar2=None, op0=ALU.mult),
                  reads=[b_stgf[i], b_const], writes=[b_stgb[i]])
            srcv = stg_b[i][:, 0:cw] if len(dst.shape) == 2 else stg_b[i][:, 0:cw].rearrange("p (a b) -> p a b", b=512)
            k.dma("sp", [(dst, srcv)], reads=[b_stgb[i]], sembuf=b_stgb[i])
        k.wait_all("sp", b_stgb + b_stgf + [b_pro, b_const, b_big])

        def combined_plan(with_next):
            plan = []
            if with_next:
                plan.append(("s", 0))
            si = 1
            c2_done = not with_next
            for ui in range(64):
                plan.append(("u", ui))
                if with_next and ui % S5_EVERY == S5_OFF and si <= 16:
                    plan.append(("s", si))
                    si += 1
                    if si == 17:
                        plan.append(("c2a",))
                elif with_next and si > 16 and not c2_done and ui % 2 == 1 and C2_INSIDE:
                    plan.append(("c2",))
                    c2_done = True
            assert si > 16 or not with_next
            return plan

        pieces = []
        WIN_ORDER = (1, 3, 2, 0)
        for j in WIN_ORDER:
            pieces.append(("win", j))
        pieces.append(("wglu", 0))
        for blk in range(nblk):
            if blk + 1 < nblk:
                for j in WIN_ORDER:
                    pieces.append(("win", j))
            for j in range(2):
                pieces.append(("wout", j))
            for ev in combined_plan(blk + 1 < nblk):
                if ev[0] == "u":
                    ui = ev[1]
                    if ui < 32 and ui % 4 == 0:
                        pieces.append(("wup", ui // 4))
                    elif ui >= 32 and ui % 2 == 0:
                        pieces.append(("wdn", ((ui - 32) // 2) % 8))
                elif ev[0] == "c2a":
                    pieces.append(("wglu", 0))
        issued = [0]

        def piece_src(kind, j):
            if kind == "win":
                return win_s[:, j, :, :], 8
            if kind == "wglu":
                return wglu_s[:, :, :], 4
            if kind == "wout":
                return wout_s[:, j, :, :], 8
            if kind == "wup":
                return wup_s[:, j, :, :], 8
            dh, fq = divmod(j, 4)
            return wdn_s[:, dh, fq * 8:(fq + 1) * 8, :], 8

        def prefetch(upto):
            while issued[0] <= min(upto, len(pieces) - 1):
                i = issued[0]
                kind, j = pieces[i]
                src, nk = piece_src(kind, j)
                slot = i % NR
                h_ = nk // 2
                k.dma("sp", [(ring[slot][:, 0:h_, :], src[:, 0:h_, :]), (ring[slot][:, h_:nk, :], src[:, h_:nk, :])], writes=[b_ring[slot]])
                issued[0] += 1

        pidx = [0]

        def next_piece(kind, j, hold=0):
            i = pidx[0]
            assert pieces[i] == (kind, j), (pieces[i], kind, j)
            prefetch(i + NR - 1 - hold)
            pidx[0] += 1
            return ring[i % NR], b_ring[i % NR]

        def load_x(blk):
            par = blk % 2
            for j in range(4):
                r0 = blk * NB + j * 128
                k.dma("sp", [(xs[par][:, j, :], x[r0:r0 + 128, :])], writes=[b_xs[par][j]])

        def rstd_from_ssq(ssq_ap, out_ap, n, bst):
            A(lambda e: e.activation(out=out_ap, in_=ssq_ap, func=AF.Sqrt, scale=1.0 / n, bias=EPS), reads=[bst], writes=[bst])
            V(lambda e: e.reciprocal(out=out_ap, in_=out_ap), reads=[bst], writes=[bst])

        def tok_prep_a(par, j, scol, bst):
            xt = xs[par][:, j, :]
            bx = b_xs[par][j]
            A(lambda e: e.activation(out=junk[:], in_=xt, func=AF.Square, accum_out=stat[:, scol:scol + 1]),
              reads=[bx], writes=[bst, b_sq[0], b_sq[1]])
            rstd_from_ssq(stat[:, scol:scol + 1], stat[:, scol + 1:scol + 2], float(D), bst)
            V(lambda e: e.tensor_scalar(out=xbn[j % 2][:], in0=xt, scalar1=stat[:, scol + 1:scol + 2], scalar2=None, op0=ALU.mult),
              reads=[bx, bst], writes=[b_xbn[j % 2]])

        def tok_prep_b(j, pt=0):
            for kk in range(8):
                k.op("pe", lambda e, kk=kk: e.transpose(PT[pt][:, kk, :], xbn[j % 2][:, kk * 128:(kk + 1) * 128], ident_bf[:]),
                     reads=[b_xbn[j % 2], b_const], writes=[b_PT[pt]], signal=(kk == 7))
            V(lambda e: e.tensor_copy(out=xT[:, :, j * 128:(j + 1) * 128], in_=PT[pt][:]),
              reads=[b_PT[pt]], writes=[b_xT])

        def proj_fm(wt, bw, m, pi):
            for kk in range(8):
                k.op("pe", lambda e, kk=kk: e.matmul(PS[pi][:], lhsT=wt[:, kk, m * 128:(m + 1) * 128], rhs=xT[:, kk, :],
                                                    start=(kk == 0), stop=(kk == 7)),
                     reads=[bw, b_xT], writes=[b_PS[pi]], signal=(kk == 7))

        psrot = [0]

        def nps():
            i = psrot[0] % 7
            psrot[0] += 1
            return i

        MIX = b_th + b_z + b_bg + [b_tcv]

        def tok_steps(par_, col0, bsts, pt=0):
            a = lambda j: (lambda: tok_prep_a(par_, j, col0 + 2 * j, bsts[j]))
            b = lambda j: (lambda: tok_prep_b(j, pt))
            return [[a(0), a(1)], [b(0), a(2)], [b(1), a(3)], [b(2)], [b(3)]]

        def stage_A(blk, pt=0):
            for grp in tok_steps(blk % 2, 0, b_stat[0:4], pt):
                for f_ in grp:
                    f_()

        def stage_B0(blk):
            wt, bw = next_piece("win", 1)
            for m in range(4):
                pi = nps()
                proj_fm(wt, bw, m, pi)
                A(lambda e, m=m, pi=pi: e.activation(out=th[:, m, :], in_=PS[pi][:], func=AF.Copy),
                  reads=[b_PS[pi]], writes=[b_th[m]] + ((b_h1 + b_lstgf + b_lstgb) if m == 0 else []))
            wt, bw = next_piece("win", 3)
            for m in range(4):
                pi = nps()
                proj_fm(wt, bw, m, pi)
                V(lambda e, m=m, pi=pi: e.tensor_tensor(out=zb[:, m, 2:514], in0=PS[pi][:], in1=th[:, m, :], op=ALU.mult),
                  reads=[b_PS[pi], b_th[m]], writes=[b_z[m]])
                G(lambda e, m=m: e.tensor_copy(out=zb[:, m, 0:2], in_=zhalo[:, m, :]), reads=[b_zh], writes=[b_z[m]])
                cw = lambda jj, m=m: pV_t[:, 4 + m * 3 + jj: 5 + m * 3 + jj]
                V(lambda e, m=m, cw=cw: e.tensor_scalar(out=th[:, m, :], in0=zb[:, m, 0:512], scalar1=cw(0), scalar2=None, op0=ALU.mult),
                  reads=[b_z[m], b_const], writes=[b_th[m]])
                V(lambda e, m=m, cw=cw: e.scalar_tensor_tensor(out=th[:, m, :], in0=zb[:, m, 1:513], scalar=cw(1), in1=th[:, m, :], op0=ALU.mult, op1=ALU.add),
                  reads=[b_z[m], b_const], writes=[b_th[m]])
                V(lambda e, m=m, cw=cw: e.scalar_tensor_tensor(out=th[:, m, :], in0=zb[:, m, 2:514], scalar=cw(2), in1=th[:, m, :], op0=ALU.mult, op1=ALU.add),
                  reads=[b_z[m], b_const], writes=[b_th[m]])
                G(lambda e, m=m: e.tensor_copy(out=zhalo[:, m, :], in_=zb[:, m, 512:514]), reads=[b_z[m]], writes=[b_zh])

        def stage_B1(blk):
            wt, bw = next_piece("win", 2)
            for m in range(4):
                pi = nps()
                proj_fm(wt, bw, m, pi)
                A(lambda e, m=m, pi=pi: e.activation(out=bg[:, m, :], in_=PS[pi][:], func=AF.Copy),
                  reads=[b_PS[pi]], writes=[b_bg[m]])
                V(lambda e, m=m: e.tensor_tensor(out=yc[:, m, :], in0=th[:, m, :], in1=bg[:, m, :], op=ALU.mult),
                  reads=[b_bg[m]], writes=[b_yc[m]])
            wt, bw = next_piece("win", 0)
            for m in range(4):
                pi = nps()
                proj_fm(wt, bw, m, pi)
                A(lambda e, m=m, pi=pi: e.activation(out=ub[:, m, :], in_=PS[pi][:], func=AF.Copy),
                  reads=[b_PS[pi]], writes=[b_ub])

        def stage_B2(blk):
            pass

        def stage_B3(blk):
            pn = nps()
            for m in range(4):
                A(lambda e, m=m: e.activation(out=sq[m % 2], in_=yc[:, m, :], func=AF.Square), reads=[b_yc[m]], writes=[b_sq[m % 2]])
                k.op("pe", lambda e, m=m: e.matmul(PS[pn][:], lhsT=ones_bf[:], rhs=sq[m % 2], start=(m == 0), stop=(m == 3)),
                     reads=[b_sq[m % 2], b_const], writes=[b_PS[pn]], signal=True)
            A(lambda e: e.activation(out=rcs[0][:], in_=PS[pn][:], func=AF.Sqrt, scale=1.0 / 512, bias=EPS), reads=[b_PS[pn]], writes=[b_rcs[0]])
            V(lambda e: e.reciprocal(out=rcs[0][:], in_=rcs[0][:]), reads=[b_rcs[0]], writes=[b_rcs[0]])
            for m in range(4):
                G(lambda e, m=m: e.tensor_tensor(out=yn[:, 4 + m, :], in0=yc[:, m, :], in1=rcs[0][:], op=ALU.mult),
                  reads=[b_yc[m], b_rcs[0]], writes=[b_yn[4 + m]])

        P1v = PS[5][:].rearrange("p (a b) -> p a b", a=4)
        P2v = PS[6][:].rearrange("p (a b) -> p a b", a=4)
        pysl = [PS[7][:, 0:128], PS[7][:, 128:256]]
        pcv = PS[7][:, 256:264]

        def s5_B(s_):
            f, hb = divmod(s_, 8)
            t, hf = divmod(hb, 2)
            rows = slice(64 * hf, 64 * hf + 64)
            for gq in range(4):
                for var, (Pv, bi) in enumerate(((P1v, 5), (P2v, 6))):
                    for sl in range(2):
                        tok = slice(f * 256 + sl, (f + 1) * 256, 2)
                        k.op("pe", lambda e, gq=gq, var=var, Pv=Pv, sl=sl, tok=tok: e.matmul(
                            Pv[:, gq, :], lhsT=Bp[rows, sl, t, gq, var, :], rhs=ub[rows, t, tok], start=(sl == 0), stop=(sl == 1)),
                             reads=[b_ub, b_const], writes=[b_PS[bi]], signal=(gq == 3 and var == 1 and sl == 1))

        def s5_mod(s_):
            f, hb = divmod(s_, 8)
            pr = hb % 2
            g0 = 4 * hb
            V(lambda e: e.tensor_tensor(out=tmpm[pr], in0=P2v, in1=Es[:, g0:g0 + 4, :], op=ALU.mult),
              reads=[b_PS[6], b_const], writes=[b_tmpm[pr]])
            V(lambda e: e.tensor_tensor(out=Vt[pr], in0=P1v, in1=Ec[:, g0:g0 + 4, :], op=ALU.mult),
              reads=[b_PS[5], b_const], writes=[b_Vt[pr]])

        def s5_scan(s_):
            f, hb = divmod(s_, 8)
            t, hf = divmod(hb, 2)
            pr = hb % 2
            g0 = 4 * hb
            V(lambda e: e.tensor_tensor(out=Vt[pr], in0=Vt[pr], in1=tmpm[pr], op=ALU.add),
              reads=[b_Vt[pr], b_tmpm[pr]], writes=[b_Vt[pr]])
            for gq in range(4):
                g = g0 + gq
                V(lambda e, gq=gq, g=g: e.tensor_tensor_scan(out=Wt[pr][:, gq, :], data0=rho[:, g:g + 1].broadcast_to([128, 128]),
                                                         data1=Vt[pr][:, gq, :], initial=cb[:, g:g + 1], op0=ALU.mult, op1=ALU.add),
                  reads=[b_Vt[pr], b_cb[t], b_const], writes=[b_Wt[pr]])
            G(lambda e: e.tensor_copy(out=Q1b[pr][:, :, 0], in_=cb[:, g0:g0 + 4]), reads=[b_cb[t]], writes=[b_Q[pr]])
            G(lambda e: e.tensor_tensor(out=Q1b[pr][:, :, 1:129], in0=Wt[pr], in1=Ec[:, g0:g0 + 4, :], op=ALU.mult),
              reads=[b_Wt[pr], b_const], writes=[b_Q[pr]])
            G(lambda e: e.tensor_tensor(out=Q2b[pr][:, :, 1:129], in0=Wt[pr], in1=Es[:, g0:g0 + 4, :], op=ALU.mult),
              reads=[b_Wt[pr], b_const], writes=[b_Q[pr]])
            G(lambda e: e.tensor_copy(out=wl[:, g0:g0 + 4], in_=Wt[pr][:, :, 127]), reads=[b_Wt[pr]], writes=[b_wl[t]])
            if hf == 1:
                gs = slice(8 * t, 8 * t + 8)
                G(lambda e: e.tensor_tensor(out=cq[:, 0, :], in0=wl[:, gs], in1=Ec[:, gs, 127], op=ALU.mult), reads=[b_wl[t], b_const], writes=[b_cq])
                G(lambda e: e.tensor_tensor(out=cq[:, 1, :], in0=wl[:, gs], in1=Es[:, gs, 127], op=ALU.mult), reads=[b_wl[t], b_const], writes=[b_cq])
                G(lambda e: e.tensor_copy(out=cqb[:, 0:2, :], in_=cq[:, 0:2, :]), reads=[b_cq], writes=[b_cqb])
                G(lambda e: e.tensor_tensor(out=cq[:, 2:4, :], in0=cq[:, 0:2, :], in1=cqb[:, 0:2, :], op=ALU.subtract), reads=[b_cq, b_cqb], writes=[b_cq])
                G(lambda e: e.tensor_copy(out=cqb[:, 2:4, :], in_=cq[:, 2:4, :]), reads=[b_cq], writes=[b_cqb])

        def s5_C(s_):
            f, hb = divmod(s_, 8)
            t, hf = divmod(hb, 2)
            pr = hb % 2
            g0 = 4 * hb
            rows = slice(64 * hf, 64 * hf + 64)
            for sl in range(2):
                qs = slice(0, 128) if sl == 0 else slice(1, 129)
                tok = slice(f * 256 + sl, (f + 1) * 256, 2)
                py = pysl[sl]
                for gq in range(4):
                    g = g0 + gq
                    k.op("pe", lambda e, gq=gq, g=g, sl=sl, qs=qs, py=py: e.matmul(py[rows, :], lhsT=Cp[:, sl, g, 0, :], rhs=Q1b[pr][:, gq, qs], start=(gq == 0), stop=False),
                         reads=[b_Q[pr], b_const], writes=[b_PS[7]], signal=False)
                    k.op("pe", lambda e, gq=gq, g=g, sl=sl, qs=qs, py=py: e.matmul(py[rows, :], lhsT=Cp[:, sl, g, 1, :], rhs=Q2b[pr][:, gq, qs], start=False, stop=False),
                         reads=[b_Q[pr], b_const], writes=[b_PS[7]], signal=False)
                k.op("pe", lambda e, sl=sl, tok=tok, py=py: e.matmul(py[rows, :], lhsT=Dm[rows, sl, t, 64 * hf:64 * hf + 64], rhs=ub[rows, t, tok], start=False, stop=True),
                     reads=[b_ub, b_const], writes=[b_PS[7]], signal=True)
            if hf == 1:
                gs = slice(8 * t, 8 * t + 8)
                k.op("pe", lambda e: e.matmul(pcv, lhsT=ident_bf[:], rhs=cqb[:, 0, :], start=True, stop=False), reads=[b_cqb, b_const], writes=[b_PS[7]], signal=False)
                k.op("pe", lambda e: e.matmul(pcv, lhsT=ident_bf[:], rhs=cqb[:, 2, :], start=False, stop=False), reads=[b_cqb, b_const], writes=[b_PS[7]], signal=False)
                k.op("pe", lambda e: e.matmul(pcv, lhsT=swap_bf[:], rhs=cqb[:, 1, :], start=False, stop=False), reads=[b_cqb, b_const], writes=[b_PS[7]], signal=False)
                k.op("pe", lambda e: e.matmul(pcv, lhsT=swap_bf[:], rhs=cqb[:, 3, :], start=False, stop=True), reads=[b_cqb, b_const], writes=[b_PS[7]], signal=True)
                A(lambda e: e.activation(out=cb[:, gs], in_=pcv, func=AF.Copy), reads=[b_PS[7]], writes=[b_cb[t]])
                for sl in range(2):
                    tok = slice(f * 256 + sl, (f + 1) * 256, 2)
                    A(lambda e, sl=sl, tok=tok: e.activation(out=yg32[:, t, tok], in_=pysl[sl], func=AF.Gelu_apprx_tanh), reads=[b_PS[7]], writes=[b_yg[t]])
                    A(lambda e, sl=sl, tok=tok: e.activation(out=ygb[:, t, tok], in_=pysl[sl], func=AF.Gelu_apprx_tanh), reads=[b_PS[7]], writes=[b_ygb] + b_xbn)

        NS = 16

        def s5_thunks():
            th_ = [lambda: s5_B(0)]
            for s_ in range(NS):
                def step(s_=s_):
                    s5_mod(s_)
                    if s_ + 1 < NS:
                        s5_B(s_ + 1)
                    s5_scan(s_)
                    if s_ >= 1:
                        s5_C(s_ - 1)
                    if s_ == NS - 1:
                        s5_C(NS - 1)
                th_.append(step)
            return th_

        GLU_BANKS = (4, 5, 6, 7)

        def stage_C2a(blk):
            wg, bwg = next_piece("wglu", 0)
            for jt in range(4):
                pi = GLU_BANKS[jt]
                for ft in range(4):
                    k.op("pe", lambda e, ft=ft, jt=jt, pi=pi: e.matmul(PS[pi][:], lhsT=wg[:, ft, jt * 128:(jt + 1) * 128], rhs=ygb[:, ft, :],
                                                                      start=(ft == 0), stop=(ft == 3)),
                         reads=[bwg, b_ygb] + b_xbn, writes=[b_PS[pi]], signal=(ft == 3))

        def stage_C2b(blk):
            pn = GLU_BANKS[0]
            for jt in range(4):
                pi = GLU_BANKS[jt]
                A(lambda e, jt=jt, pi=pi: e.activation(out=sg[jt % 2][:], in_=PS[pi][:], func=AF.Sigmoid), reads=[b_PS[pi]], writes=[b_sg[jt % 2]])
                V(lambda e, jt=jt: e.tensor_tensor(out=yg32[:, jt, :], in0=yg32[:, jt, :], in1=sg[jt % 2][:], op=ALU.mult),
                  reads=[b_yg[jt], b_sg[jt % 2]], writes=[b_yg[jt]])
                A(lambda e, jt=jt: e.activation(out=sq[jt % 2], in_=yg32[:, jt, :], func=AF.Square), reads=[b_yg[jt]], writes=[b_sq[jt % 2]])
                k.op("pe", lambda e, jt=jt: e.matmul(PS[pn][:], lhsT=ones_bf[:], rhs=sq[jt % 2], start=(jt == 0), stop=(jt == 3)),
                     reads=[b_sq[jt % 2], b_const], writes=[b_PS[pn]], signal=True)
            A(lambda e: e.activation(out=rcs[1][:], in_=PS[pn][:], func=AF.Sqrt, scale=1.0 / 512, bias=EPS), reads=[b_PS[pn]], writes=[b_rcs[1]])
            V(lambda e: e.reciprocal(out=rcs[1][:], in_=rcs[1][:]), reads=[b_rcs[1]], writes=[b_rcs[1]])
            for jt in range(4):
                V(lambda e, jt=jt: e.tensor_tensor(out=yn[:, jt, :], in0=yg32[:, jt, :], in1=rcs[1][:], op=ALU.mult),
                  reads=[b_yg[jt], b_rcs[1]], writes=[b_yn[jt]])

        def stage_D(blk):
            par = blk % 2
            wo0, bwo0 = next_piece("wout", 0)
            wo1, bwo1 = next_piece("wout", 1, hold=1)
            for j in range(4):
                js = slice(j * 128, (j + 1) * 128)
                pis = []
                for dh, (wo, bwo) in enumerate(((wo0, bwo0), (wo1, bwo1))):
                    pi = nps()
                    pis.append(pi)
                    for ft in range(8):
                        k.op("pe", lambda e, ft=ft, pi=pi, wo=wo: e.matmul(PS[pi][:], lhsT=yn[:, ft, js], rhs=wo[:, ft, :], start=(ft == 0), stop=(ft == 7)),
                             reads=[bwo, b_yn[ft]], writes=[b_PS[pi]], signal=(ft == 7))
                    A(lambda e, pi=pi, dh=dh, j=j: e.activation(out=junk[:, 0:512], in_=PS[pi][:], func=AF.Square, accum_out=stat[:, 32 + 4 * j + dh:33 + 4 * j + dh]),
                      reads=[b_PS[pi]], writes=[b_statD[j], b_sq[0]])
                c0 = 32 + 4 * j
                V(lambda e, c0=c0: e.tensor_tensor(out=stat[:, c0 + 2:c0 + 3], in0=stat[:, c0:c0 + 1], in1=stat[:, c0 + 1:c0 + 2], op=ALU.add), reads=[b_statD[j]], writes=[b_statD[j]])
                rstd_from_ssq(stat[:, c0 + 2:c0 + 3], stat[:, c0 + 3:c0 + 4], float(D), b_statD[j])
                for dh in range(2):
                    pi = pis[dh]
                    ds = slice(dh * 512, (dh + 1) * 512)
                    V(lambda e, pi=pi, ds=ds, dh=dh: e.tensor_tensor(out=tmpE[dh][:], in0=PS[pi][:], in1=gpm_bc[:, ds], op=ALU.mult),
                      reads=[b_PS[pi], b_const], writes=[b_tmpE[dh]])
                    V(lambda e, ds=ds, dh=dh, c0=c0, j=j: e.scalar_tensor_tensor(out=xs[par][:, j, ds], in0=tmpE[dh][:], scalar=stat[:, c0 + 3:c0 + 4], in1=xs[par][:, j, ds],
                                                                   op0=ALU.mult, op1=ALU.add),
                      reads=[b_tmpE[dh], b_statD[j], b_xs[par][j]], writes=[b_xs[par][j]])

        def stage_E0(blk):
            for grp in tok_steps(blk % 2, 8, b_stat[4:8]):
                for f_ in grp:
                    f_()

        def mlp_units(blk):
            par = blk % 2
            units = []
            st_ = {}

            def up(q, tt):
                if tt == 0:
                    st_["wu"] = next_piece("wup", q)
                wu, bwu = st_["wu"]
                ff = q * 4 + tt
                pi = ff % 5
                proj_fm(wu, bwu, tt, pi)
                A(lambda e: e.activation(out=rl[ff % 2][:], in_=PS[pi][:], func=AF.Relu), reads=[b_PS[pi]], writes=[b_rl[ff % 2]])
                G(lambda e: e.tensor_tensor(out=h1T[:, ff, :], in0=rl[ff % 2][:], in1=rl[ff % 2][:], op=ALU.mult),
                  reads=[b_rl[ff % 2]], writes=[b_h1[ff]] + ((MIX + b_lstgf + b_lstgb) if ff == 0 else []))

            def down(p_, dh, fq, jj):
                j = 2 * p_ + jj
                bank = 2 * jj + dh
                if jj == 0:
                    st_["wd"] = next_piece("wdn", dh * 4 + fq)
                wd, bwd = st_["wd"]
                js = slice(j * 128, (j + 1) * 128)
                for kk in range(8):
                    ff = fq * 8 + kk
                    k.op("pe", lambda e, kk=kk, ff=ff: e.matmul(
                        PS[bank][:], lhsT=h1T[:, ff, js], rhs=wd[:, kk, :], start=(fq == 0 and kk == 0), stop=(fq == 3 and kk == 7)),
                         reads=[bwd, b_h1[ff]], writes=[b_PS[bank]], signal=(kk == 7))
                if fq == 3:
                    c0 = 48 + 4 * j
                    A(lambda e: e.activation(out=junk[:, 0:512], in_=PS[bank][:], func=AF.Square, accum_out=stat[:, c0 + dh:c0 + dh + 1]),
                      reads=[b_PS[bank]], writes=[b_statE[j], b_sq[0]])
                    if dh == 1:
                        V(lambda e: e.tensor_tensor(out=stat[:, c0 + 2:c0 + 3], in0=stat[:, c0:c0 + 1], in1=stat[:, c0 + 1:c0 + 2], op=ALU.add),
                          reads=[b_statE[j]], writes=[b_statE[j]])
                        rstd_from_ssq(stat[:, c0 + 2:c0 + 3], stat[:, c0 + 3:c0 + 4], float(D), b_statE[j])
                        for hh in range(2):
                            bk = 2 * jj + hh
                            ds = slice(hh * 512, (hh + 1) * 512)
                            V(lambda e, bk=bk, ds=ds, hh=hh: e.tensor_tensor(out=tmpE[hh][:], in0=PS[bk][:], in1=gpo_bc[:, ds], op=ALU.mult),
                              reads=[b_PS[bk], b_const], writes=[b_tmpE[hh]])
                            V(lambda e, ds=ds, hh=hh: e.scalar_tensor_tensor(out=xs[par][:, j, ds], in0=tmpE[hh][:], scalar=stat[:, c0 + 3:c0 + 4], in1=xs[par][:, j, ds],
                                                                      op0=ALU.mult, op1=ALU.add),
                              reads=[b_tmpE[hh], b_statE[j], b_xs[par][j]], writes=[b_xs[par][j]])
                        r0 = blk * NB + j * 128
                        k.dma("sp", [(out[r0:r0 + 128, :], xs[par][:, j, :])], reads=[b_xs[par][j]], sembuf=b_xs[par][j])

            for q in range(8):
                for tt in range(4):
                    units.append(lambda q=q, tt=tt: up(q, tt))
            for p_ in range(2):
                for dh in range(2):
                    for fq in range(4):
                        for jj in range(2):
                            units.append(lambda p_=p_, dh=dh, fq=fq, jj=jj: down(p_, dh, fq, jj))
            return units

        load_x(0)
        if nblk > 1:
            load_x(1)
        stage_A(0)
        stage_B0(0)
        stage_B1(0)
        stage_B2(0)
        stage_B3(0)
        for f_ in s5_thunks():
            f_()
            late_convert(4)
        late_convert(len(late_chunks))
        k.wait_all("sp", b_lstgb + b_lstgf)
        stage_C2a(0)
        for blk in range(nblk):
            has_next = blk + 1 < nblk
            if has_next:
                stage_A(blk + 1, pt=1)
            stage_C2b(blk)
            if has_next:
                stage_B0(blk + 1)
                stage_B1(blk + 1)
                stage_B2(blk + 1)
            stage_D(blk)
            stage_E0(blk)
            if has_next:
                stage_B3(blk + 1)
            units = mlp_units(blk)
            s5 = s5_thunks() if has_next else None
            for ev in combined_plan(has_next):
                if ev[0] == "u":
                    units[ev[1]]()
                elif ev[0] == "s":
                    s5[ev[1]]()
                elif ev[0] == "c2a":
                    stage_C2a(blk + 1)
            if blk + 2 < nblk:
                load_x(blk + 2)

        if _STOP:
            for en in ("act", "dve", "pe", "pool"):
                k.wait_all("sp", [b_PS[i] for i in range(7)] + b_yn + b_yg + b_Q + b_cb + [b_xT, b_ub])
        k.wait_all("sp", b_xs[0] + b_xs[1])
        _build.stats = dict(k.ninstr)
    return nc


def _host_params(inp):
    f = np.float32
    lam_re = np.asarray(inp["lam_re"], f)[0]
    lam_im = np.asarray(inp["lam_im"], f)[0]
    log_dt = np.asarray(inp["log_dt"], f)[0]
    b_re = np.asarray(inp["b_re"], f)[0]
    b_im = np.asarray(inp["b_im"], f)[0]
    c_re = np.asarray(inp["c_re"], f)[0]
    c_im = np.asarray(inp["c_im"], f)[0]
    d_skip = np.asarray(inp["d_skip"], f)[0]
    conv_w = np.asarray(inp["conv_w"], f)[0]

    pA = np.zeros((128, 225), f)
    pA[:, 0:32] = np.concatenate([lam_re.T, lam_re.T], axis=0)
    pA[:, 32:64] = np.concatenate([lam_im.T, lam_im.T], axis=0)
    pA[:, 64:96] = np.broadcast_to(log_dt[None, :], (128, 32))
    pA[0:64, 96] = 1.0
    pA[64:128, 96] = -1.0
    pA[:, 97:225] = np.broadcast_to(np.arange(1, 129, dtype=f)[None, :], (128, 128))

    def orientB(a_gp):
        a = a_gp.reshape(4, 8, 64)
        a = np.broadcast_to(a[:, :, None, :], (4, 8, 16, 64))
        return np.ascontiguousarray(a.transpose(1, 2, 0, 3)).reshape(128, 4 * 64)

    def orientB_b(b_gph):
        a = b_gph.reshape(4, 8, 64, 16)
        return np.ascontiguousarray(a.transpose(1, 3, 0, 2)).reshape(128, 4 * 64)

    pB = np.zeros((128, 5, 256), f)
    pB[:, 0] = orientB(lam_re)
    pB[:, 1] = orientB(lam_im)
    pB[:, 2] = orientB(np.broadcast_to(log_dt[:, None], (32, 64)))
    pB[:, 3] = orientB_b(b_re)
    pB[:, 4] = orientB_b(b_im)

    crT = np.ascontiguousarray(c_re.transpose(2, 0, 1))
    ciT = np.ascontiguousarray(c_im.transpose(2, 0, 1))
    pC = np.zeros((128, 32, 2, 16), f)
    pC[0:64, :, 0, :] = crT
    pC[64:128, :, 0, :] = ciT
    pC[0:64, :, 1, :] = ciT
    pC[64:128, :, 1, :] = crT

    pV = np.zeros((128, 64), f)
    pV[:, 0:4] = d_skip.reshape(4, 8 * 16).T
    for t in range(4):
        for j in range(3):
            pV[:, 4 + t * 3 + j] = conv_w[j, t * 128:(t + 1) * 128]
    pV[:, 16:24] = np.asarray(inp["g_pre_mix"], f)[0].reshape(8, 128).T
    pV[:, 24:32] = np.asarray(inp["g_pre_mlp"], f)[0].reshape(8, 128).T
    gcat = np.concatenate([np.asarray(inp["g_ssm_out"], f)[0], np.asarray(inp["g_conv_out"], f)[0]])
    pV[:, 32:40] = gcat.reshape(8, 128).T
    r = np.arange(128)
    for gq in range(4):
        pV[:, 40 + gq] = ((r // 16) % 4 == gq).astype(f)

    gvec = np.stack([np.asarray(inp["g_post_mix"], f)[0], np.asarray(inp["g_post_mlp"], f)[0]], axis=0)

    cst = np.zeros((128, 3, 128), f)
    cst[:, 0, :] = np.eye(128, dtype=f)
    rr = np.arange(128)
    cst[:, 2, :] = (rr[:, None] // 16 == rr[None, :] // 16).astype(f)
    for p in range(64):
        cst[64 + p, 1, p] = -1.0
        cst[p, 1, 64 + p] = 1.0
    return dict(pA=pA, pB=pB, pC=pC, pV=pV, gvec=np.ascontiguousarray(gvec), cst=cst)


def _shared_inputs(inp):
    f = np.float32
    d = _host_params(inp)
    d["w_in"] = np.ascontiguousarray(np.asarray(inp["w_in"], f)[0])
    d["w_glu"] = np.ascontiguousarray(np.asarray(inp["w_glu"], f)[0])
    d["w_out"] = np.ascontiguousarray(np.asarray(inp["w_out"], f)[0])
    d["w_up"] = np.ascontiguousarray(np.asarray(inp["w_up"], f)[0])
    d["w_down"] = np.ascontiguousarray(np.asarray(inp["w_down"], f)[0])
    return d


_NC_CACHE = {}


def kernel(**inputs):
    x = np.asarray(inputs["x"], np.float32)
    nb, L, _ = x.shape
    nblk = L // NB
    key = nblk
    if key not in _NC_CACHE:
        _NC_CACHE[key] = _build(nblk)
    nc = _NC_CACHE[key]
    shared = _shared_inputs(inputs)
    in_maps = []
    for c in range(nb):
        m = dict(shared)
        m["x"] = np.ascontiguousarray(x[c])
        in_maps.append(m)
    res = run_bass_kernel_spmd(nc, in_maps, core_ids=list(range(nb)))
    return np.stack([np.asarray(r["out"], np.float32) for r in res.results], axis=0)
```

```python
import math
from contextlib import ExitStack

import numpy as np
import concourse.bass as bass
import concourse.mybir as mybir
from concourse.bass_utils import run_bass_kernel_spmd

F32 = mybir.dt.float32
BF16 = mybir.dt.bfloat16
AF = mybir.ActivationFunctionType
ALU = mybir.AluOpType

D = 1024
SEQ = 8192
NB = 512
NCORES = 8
C2_INSIDE = False
S5_EVERY = 4
S5_OFF = 1
EPS = 1e-6
MAGIC = 12582912.0
TWO_PI = 2.0 * math.pi
C1 = 6.28125
C2 = TWO_PI - C1
PI_LO = 3.1415925


class Ev:
    __slots__ = ("key", "sem", "n")

    def __init__(self, key, sem, n):
        self.key, self.sem, self.n = key, sem, n


class Buf:
    def __init__(self, name):
        self.name = name
        self.last_w = None
        self.reads = []
        self.dsem = None
        self.dcnt = 0


class KB:
    ENG = ("pe", "act", "dve", "pool", "sp")

    def __init__(self, nc, stack):
        self.nc = nc
        self.stack = stack
        self.e = {"pe": nc.tensor, "act": nc.scalar, "dve": nc.vector, "pool": nc.gpsimd, "sp": nc.sync}
        self.sem = {}
        self.cnt = {}
        for k in ("pe", "act", "dve", "pool"):
            self.sem[k] = stack.enter_context(nc.semaphore("s_" + k))
            self.cnt[k] = 0
        self.waited = {k: {} for k in self.ENG}
        self.pending = {k: [] for k in self.ENG}
        self.ninstr = {k: 0 for k in self.ENG}

    def buf(self, name):
        return Buf(name)

    def bufs(self, name, n):
        return [Buf(f"{name}{i}") for i in range(n)]

    def _wait(self, e, deps):
        need = {}
        for d in deps:
            if d is None:
                continue
            if d.key == e and e == "pe":
                continue
            if d.n is None:
                raise RuntimeError(f"wait on unsignalled event of {d.key} from {e}")
            cur = need.get(d.key)
            if cur is None or d.n > cur[1]:
                need[d.key] = (d.sem, d.n)
        for key, (sem, n) in need.items():
            if self.waited[e].get(key, 0) >= n:
                continue
            self.e[e].wait_ge(sem, n)
            self.waited[e][key] = n

    @staticmethod
    def _deps(reads, writes):
        deps = []
        for b in reads:
            deps.append(b.last_w)
        for b in writes:
            deps.append(b.last_w)
            deps.extend(b.reads)
        return deps

    def op(self, e, fn, reads=(), writes=(), signal=True):
        self._wait(e, self._deps(reads, writes))
        ins = fn(self.e[e])
        self.ninstr[e] += 1
        if signal:
            self.cnt[e] += 1
            ins.then_inc(self.sem[e], 1)
            ev = Ev(e, self.sem[e], self.cnt[e])
            for p in self.pending[e]:
                p.n = self.cnt[e]
            self.pending[e] = []
        else:
            ev = Ev(e, self.sem[e], None)
            self.pending[e].append(ev)
        for b in reads:
            b.reads.append(ev)
        for b in writes:
            b.last_w = ev
            b.reads = []
        return ev

    def dma(self, q, outs_ins, reads=(), writes=(), sembuf=None):
        self._wait(q, self._deps(reads, writes))
        sb = sembuf or (writes[0] if writes else reads[0])
        if sb.dsem is None:
            sb.dsem = self.stack.enter_context(self.nc.semaphore("d_" + sb.name))
        for (o, i) in outs_ins:
            self.e[q].dma_start(out=o, in_=i).then_inc(sb.dsem, 16)
            sb.dcnt += 1
            self.ninstr[q] += 1
        ev = Ev("d_" + sb.name, sb.dsem, 16 * sb.dcnt)
        for b in reads:
            b.reads.append(ev)
        for b in writes:
            b.last_w = ev
            b.reads = []
        return ev

    def wait_all(self, e, bufs):
        deps = []
        for b in bufs:
            deps.append(b.last_w)
            deps.extend(b.reads)
        self._wait(e, deps)


_STOP = None


def _build(nblk, dbg=False):
    nc = bass.Bass("TRN2", target_bir_lowering=False)
    L = nblk * NB

    def din(name, shape, dt=F32):
        return nc.dram_tensor(name, list(shape), dt, kind="ExternalInput").ap()

    x = din("x", [L, D])
    w_in = din("w_in", [D, 2048])
    w_glu = din("w_glu", [512, 512])
    w_out = din("w_out", [D, D])
    w_up = din("w_up", [D, 4096])
    w_down = din("w_down", [4096, D])
    pA = din("pA", [128, 225])
    pB = din("pB", [128, 5, 256])
    pC = din("pC", [128, 32, 2, 16])
    pV = din("pV", [128, 64])
    gvec = din("gvec", [2, D])
    cst = din("cst", [128, 3, 128])
    out = nc.dram_tensor("out", [L, D], F32, kind="ExternalOutput").ap()
    if dbg:
        dbg_o = nc.dram_tensor("dbg", [128, 16, 512], F32, kind="ExternalOutput").ap()

    win_s = nc.dram_tensor("win_s", [128, 4, 8, 512], BF16).ap()
    wglu_s = nc.dram_tensor("wglu_s", [128, 4, 512], BF16).ap()
    wout_s = nc.dram_tensor("wout_s", [128, 2, 8, 512], BF16).ap()
    wup_s = nc.dram_tensor("wup_s", [128, 8, 8, 512], BF16).ap()
    wdn_s = nc.dram_tensor("wdn_s", [128, 2, 32, 512], BF16).ap()

    with ExitStack() as st:
        def sb(name, shape, dt):
            return st.enter_context(nc.sbuf_tensor(name, list(shape), dt))

        def ps(name, shape, dt=F32):
            return st.enter_context(nc.psum_tensor(name, list(shape), dt))

        k = KB(nc, st)

        pA_t = sb("pA_t", [128, 225], F32)
        pV_t = sb("pV_t", [128, 64], F32)
        cst_t = sb("cst_t", [128, 3, 128], F32)
        ident_bf = sb("ident_bf", [128, 128], BF16)
        swap_bf = sb("swap_bf", [128, 128], BF16)
        ones_bf = sb("ones_bf", [128, 128], BF16)
        rho = sb("rho", [128, 32], F32)
        Ec = sb("Ec", [128, 32, 128], BF16)
        Es = sb("Es", [128, 32, 128], BF16)
        Bp = sb("Bp", [128, 2, 4, 4, 2, 128], BF16)
        Cp = sb("Cp", [128, 2, 32, 2, 64], BF16)
        Dm = sb("Dm", [128, 2, 4, 128], BF16)
        gpm_bc = sb("gpm_bc", [128, D], F32)
        gpo_bc = sb("gpo_bc", [128, D], F32)
        NR = 3
        ring = [sb(f"ring{i}", [128, 8, 512], BF16) for i in range(NR)]
        xs = [sb(f"xs{i}", [128, 4, D], F32) for i in range(2)]
        xT = sb("xT", [128, 8, NB], BF16)
        xbn2 = sb("xbn2", [128, 2, D], BF16)
        xbn = [xbn2[:, 0, :], xbn2[:, 1, :]]
        stat = sb("stat", [128, 80], F32)
        big = sb("big", [128, 32, NB], BF16)
        bigf = big[:].bitcast(F32)
        yn = sb("yn", [128, 8, NB], BF16)
        sq2 = sb("sq2", [128, 2, NB], BF16)
        sq = [sq2[:, 0, :], sq2[:, 1, :]]
        junk = sq2[:].rearrange("p a b -> p (a b)")
        rcs0 = sb("rcs0", [128, NB], F32)
        rcs = [rcs0, rcs0]
        rcs1p = sb("rcs1p", [128, 128], F32)
        s5w = sb("s5w", [128, 6, 512], F32)
        Vt = [s5w[:, i, :].rearrange("p (a b) -> p a b", a=4) for i in range(2)]
        tmpm1 = s5w[:, 2, :].rearrange("p (a b) -> p a b", a=4)
        tmpm = [tmpm1, tmpm1]
        Wt = [s5w[:, 3 + i, :].rearrange("p (a b) -> p a b", a=4) for i in range(2)]
        tmpE = [s5w[:, 5, :], None]
        s5wf = s5w[:].rearrange("p a b -> p (a b)")
        Bvv = s5wf[:, 0:2048].rearrange("p (s t v c) -> p s t v c", s=2, t=4, v=2)
        Cv = s5wf[:, 2048:3072].rearrange("p (g v h) -> p g v h", v=2, h=16)
        Q1b = [sb(f"Q1b{i}", [128, 4, 130], BF16) for i in range(2)]
        Q2b = [sb(f"Q2b{i}", [128, 4, 130], BF16) for i in range(2)]
        cb = sb("cb", [128, 32], F32)
        wl = sb("wl", [128, 32], F32)
        cq = sb("cq", [128, 6, 8], F32)
        cqb = sb("cqb", [128, 4, 8], BF16)
        yg32 = sb("yg32", [128, 4, NB], F32)
        ygb = xbn2[:].rearrange("p a (b c) -> p (a b) c", b=2)
        tmpE[1] = sb("tmpE1", [128, NB], F32)
        sg = tmpE
        zhalo = sb("zhalo", [128, 4, 2], F32)
        rl = tmpE

        bigflat = big[:].rearrange("p a b -> p (a b)")
        bigfflat = bigflat.bitcast(F32)
        ub_t = sb("ub_t", [128, 4, NB], BF16)
        ub = ub_t[:]
        th = bigfflat[:, 1024:3072].rearrange("p (a b) -> p a b", a=4)
        zb = bigfflat[:, 3072:5128].rearrange("p (a b) -> p a b", a=4)
        bg = bigfflat[:, 5128:7176].rearrange("p (a b) -> p a b", a=4)
        tcv = bigfflat[:, 7176:7688]
        yc = th
        h1T = big

        PS = [ps(f"ps{i}", [128, NB], F32) for i in range(8)]
        PT = [PS[7][:].bitcast(BF16).rearrange("p (a b) -> p a b", a=8),
              PS[0][:].bitcast(BF16).rearrange("p (a b) -> p a b", a=8)]

        b_const = k.buf("const")
        b_ring = k.bufs("ring", NR)
        b_xs = [k.bufs(f"xs{i}_", 4) for i in range(2)]
        b_xT = k.buf("xT")
        b_xbn = k.bufs("xbn", 2)
        b_junk = k.buf("junk")
        b_stat = k.bufs("stat", 8)
        b_statD = k.bufs("statD", 4)
        b_statE = k.bufs("statE", 4)
        b_big = k.buf("big")
        b_ub = k.buf("ub")
        b_th = k.bufs("th", 4)
        b_z = k.bufs("z", 4)
        b_bg = k.bufs("bg", 4)
        b_tcv = k.buf("tcv")
        b_yc = b_th
        b_h1 = k.bufs("h1T", 32)
        b_yn = k.bufs("yn", 8)
        b_sq = k.bufs("sq", 2)
        b_rcs0 = k.buf("rcs")
        b_rcs = [b_rcs0, b_rcs0]
        b_Vt = k.bufs("Vt", 2)
        b_tmpm0 = k.buf("tmpm")
        b_tmpm = [b_tmpm0, b_tmpm0]
        b_Wt = k.bufs("Wt", 2)
        b_Q = k.bufs("Q", 2)
        b_cb = k.bufs("cb", 4)
        b_wl = k.bufs("wl", 4)
        b_cq = k.buf("cq")
        b_cqb = k.buf("cqb")
        b_yg = k.bufs("yg", 4)
        b_ygb = k.buf("ygb")
        b_tmpE = k.bufs("tmpE", 2)
        b_sg = b_tmpE
        b_zh = k.buf("zhalo")
        b_rl = b_tmpE
        b_PS = k.bufs("PS", 8)
        b_PT = [b_PS[7], b_PS[0]]
        b_out = k.buf("outdram")
        b_scr = {n: k.buf("scr_" + n) for n in ("win", "wglu", "wout", "wup", "wdn")}
        b_dbg = k.buf("dbg")
        MIX_ALL = b_th + b_z + b_bg + [b_tcv]

        k.dma("sp", [(pA_t[:], pA[:, :])], writes=[b_const])
        k.dma("sp", [(pV_t[:], pV[:, :])], writes=[b_const])
        k.dma("sp", [(cst_t[:], cst[:, :, :])], writes=[b_const])
        k.dma("sp", [(gpm_bc[:], gvec[0:1, :].partition_broadcast(128))], writes=[b_const])
        k.dma("sp", [(gpo_bc[:], gvec[1:2, :].partition_broadcast(128))], writes=[b_const])
        pB_t = xs[1][:].rearrange("p a b -> p (a b)")[:, 0:1280].rearrange("p (a b) -> p a b", a=5)
        pC_t = xs[1][:].rearrange("p a b -> p (a b)")[:, 1280:2304].rearrange("p (a b c) -> p a b c", a=32, b=2)
        b_pro = k.buf("pro")
        k.dma("sp", [(pB_t, pB[:, :, :])], writes=[b_pro])
        k.dma("sp", [(pC_t, pC[:, :, :, :])], writes=[b_pro])

        def V(fn, reads=(), writes=()):
            return k.op("dve", fn, reads=reads, writes=writes)

        def A(fn, reads=(), writes=()):
            return k.op("act", fn, reads=reads, writes=writes)

        def G(fn, reads=(), writes=()):
            return k.op("pool", fn, reads=reads, writes=writes)

        C_ = [b_const]
        P_ = [b_pro]
        CP = [b_const, b_pro]

        V(lambda e: e.tensor_copy(out=ident_bf[:], in_=cst_t[:, 0, :]), reads=C_, writes=C_)
        V(lambda e: e.tensor_copy(out=swap_bf[:], in_=cst_t[:, 1, :]), reads=C_, writes=C_)
        V(lambda e: e.memset(ones_bf[:], 1.0), writes=C_)
        V(lambda e: e.memset(cb[:], 0.0), writes=b_cb)
        V(lambda e: e.memset(zhalo[:], 0.0), writes=[b_zh])
        for i_ in range(2):
            V(lambda e, i_=i_: e.memset(Q2b[i_][:], 0.0), writes=[b_Q[i_]])
            V(lambda e, i_=i_: e.memset(Q1b[i_][:], 0.0), writes=[b_Q[i_]])

        lrA = pA_t[:, 0:32]
        liA = pA_t[:, 32:64]
        ldA = pA_t[:, 64:96]
        sgnA = pA_t[:, 96:97]
        kvec = pA_t[:, 97:225]
        mask4 = pV_t[:, 40:44]

        xs0f = xs[0][:].rearrange("p a b -> p (a b)")
        tB = [xs0f[:, 256 * i:256 * (i + 1)] for i in range(16)]
        xs1f = xs[1][:].rearrange("p a b -> p (a b)")
        tAs = [xs1f[:, 2304 + 32 * i: 2304 + 32 * (i + 1)] for i in range(16)]
        Cav = xs1f[:, 2816:3840].rearrange("p (g v h) -> p g v h", v=2, h=16)

        def range_reduce(src, nbuf, dst):
            V(lambda e: e.tensor_scalar(out=nbuf, in0=src, scalar1=1.0 / TWO_PI, scalar2=MAGIC, op0=ALU.mult, op1=ALU.add), reads=CP, writes=P_)
            V(lambda e: e.tensor_scalar(out=nbuf, in0=nbuf, scalar1=-MAGIC, scalar2=None, op0=ALU.add), reads=CP, writes=P_)
            V(lambda e: e.scalar_tensor_tensor(out=dst, in0=nbuf, scalar=-C1, in1=src, op0=ALU.mult, op1=ALU.add), reads=CP, writes=P_)
            V(lambda e: e.scalar_tensor_tensor(out=dst, in0=nbuf, scalar=-C2, in1=dst, op0=ALU.mult, op1=ALU.add), reads=CP, writes=P_)
            V(lambda e: e.tensor_scalar(out=dst, in0=dst, scalar1=PI_LO, scalar2=-PI_LO, op0=ALU.min, op1=ALU.max), reads=CP, writes=P_)

        def shift_quarter(src, mbuf, dst):
            V(lambda e: e.tensor_scalar(out=dst, in0=src, scalar1=math.pi / 2, scalar2=None, op0=ALU.add), reads=CP, writes=P_)
            V(lambda e: e.tensor_scalar(out=mbuf, in0=dst, scalar1=math.pi, scalar2=-TWO_PI, op0=ALU.is_gt, op1=ALU.mult), reads=CP, writes=P_)
            V(lambda e: e.tensor_tensor(out=dst, in0=dst, in1=mbuf, op=ALU.add), reads=CP, writes=P_)
            V(lambda e: e.tensor_scalar(out=dst, in0=dst, scalar1=PI_LO, scalar2=-PI_LO, op0=ALU.min, op1=ALU.max), reads=CP, writes=P_)

        dtA, lrdtA, phiA, nA = tAs[0], tAs[1], tAs[2], tAs[3]
        A(lambda e: e.activation(out=dtA, in_=ldA, func=AF.Exp), reads=CP, writes=P_)
        V(lambda e: e.tensor_tensor(out=lrdtA, in0=lrA, in1=dtA, op=ALU.mult), reads=CP, writes=P_)
        rho1, sinA, cosA, tqA, mqA, arA, aiA, phi2 = tAs[4], tAs[5], tAs[6], tAs[7], tAs[8], tAs[9], tAs[10], tAs[11]
        A(lambda e: e.activation(out=rho1, in_=lrdtA, func=AF.Exp), reads=CP, writes=P_)
        V(lambda e: e.tensor_tensor(out=rho[:], in0=rho1, in1=rho1, op=ALU.mult), reads=CP, writes=C_)
        V(lambda e: e.tensor_tensor(out=phiA, in0=liA, in1=dtA, op=ALU.mult), reads=CP, writes=P_)
        range_reduce(phiA, nA, phiA)
        A(lambda e: e.activation(out=sinA, in_=phiA, func=AF.Sin), reads=CP, writes=P_)
        shift_quarter(phiA, mqA, tqA)
        A(lambda e: e.activation(out=cosA, in_=tqA, func=AF.Sin), reads=CP, writes=P_)
        V(lambda e: e.tensor_tensor(out=arA, in0=rho1, in1=cosA, op=ALU.mult), reads=CP, writes=P_)
        V(lambda e: e.tensor_tensor(out=aiA, in0=rho1, in1=sinA, op=ALU.mult), reads=CP, writes=P_)
        V(lambda e: e.tensor_scalar(out=phi2, in0=phiA, scalar1=2.0, scalar2=None, op0=ALU.mult), reads=CP, writes=P_)
        range_reduce(phi2, nA, phi2)
        phiA = phi2
        big1 = bigfflat[:, 0:4096].rearrange("p (g k) -> p g k", g=32)
        big2 = bigfflat[:, 4096:8192].rearrange("p (g k) -> p g k", g=32)
        b_bigp = [b_big, b_pro, b_const]
        V(lambda e: e.tensor_tensor(out=big1, in0=phiA.unsqueeze(2).broadcast_to([128, 32, 128]),
                                    in1=kvec.unsqueeze(1).broadcast_to([128, 32, 128]), op=ALU.mult),
          reads=CP, writes=[b_big])
        b1f = bigfflat[:, 0:4096]
        b2f = bigfflat[:, 4096:8192]

        def big_reduce():
            V(lambda e: e.tensor_scalar(out=b2f, in0=b1f, scalar1=1.0 / TWO_PI, scalar2=MAGIC, op0=ALU.mult, op1=ALU.add), reads=b_bigp, writes=[b_big])
            V(lambda e: e.tensor_scalar(out=b2f, in0=b2f, scalar1=-MAGIC, scalar2=None, op0=ALU.add), reads=b_bigp, writes=[b_big])
            V(lambda e: e.scalar_tensor_tensor(out=b1f, in0=b2f, scalar=-C1, in1=b1f, op0=ALU.mult, op1=ALU.add), reads=b_bigp, writes=[b_big])
            V(lambda e: e.scalar_tensor_tensor(out=b1f, in0=b2f, scalar=-C2, in1=b1f, op0=ALU.mult, op1=ALU.add), reads=b_bigp, writes=[b_big])
            V(lambda e: e.tensor_scalar(out=b1f, in0=b1f, scalar1=PI_LO, scalar2=-PI_LO, op0=ALU.min, op1=ALU.max), reads=b_bigp, writes=[b_big])

        big_reduce()
        A(lambda e: e.activation(out=Es[:].rearrange("p g k -> p (g k)"), in_=b1f, func=AF.Sin), reads=b_bigp, writes=C_)
        V(lambda e: e.tensor_scalar(out=b1f, in0=b1f, scalar1=math.pi / 2, scalar2=None, op0=ALU.add), reads=b_bigp, writes=[b_big])
        V(lambda e: e.tensor_scalar(out=b2f, in0=b1f, scalar1=math.pi, scalar2=-TWO_PI, op0=ALU.is_gt, op1=ALU.mult), reads=b_bigp, writes=[b_big])
        V(lambda e: e.tensor_tensor(out=b1f, in0=b1f, in1=b2f, op=ALU.add), reads=b_bigp, writes=[b_big])
        V(lambda e: e.tensor_scalar(out=b1f, in0=b1f, scalar1=PI_LO, scalar2=-PI_LO, op0=ALU.min, op1=ALU.max), reads=b_bigp, writes=[b_big])
        A(lambda e: e.activation(out=Ec[:].rearrange("p g k -> p (g k)"), in_=b1f, func=AF.Sin), reads=b_bigp, writes=C_)

        def pBv(i):
            return pB_t[:, i, :]
        lrB, liB, ldB, brB, biB = (pBv(i) for i in range(5))
        dtB, lrdtB, magB, phiB, nB_, sinB, cosB, mB = tB[0], tB[1], tB[2], tB[3], tB[4], tB[5], tB[6], tB[7]
        A(lambda e: e.activation(out=dtB, in_=ldB, func=AF.Exp), reads=CP, writes=P_)
        V(lambda e: e.tensor_tensor(out=lrdtB, in0=lrB, in1=dtB, op=ALU.mult), reads=CP, writes=P_)
        A(lambda e: e.activation(out=magB, in_=lrdtB, func=AF.Exp), reads=CP, writes=P_)
        V(lambda e: e.tensor_tensor(out=phiB, in0=liB, in1=dtB, op=ALU.mult), reads=CP, writes=P_)
        range_reduce(phiB, nB_, phiB)
        A(lambda e: e.activation(out=sinB, in_=phiB, func=AF.Sin), reads=CP, writes=P_)
        shift_quarter(phiB, mB, phiB)
        A(lambda e: e.activation(out=cosB, in_=phiB, func=AF.Sin), reads=CP, writes=P_)
        abr, abi, den, t1, t2, crB, ciB = tB[0], tB[1], tB[3], tB[4], tB[7], tB[8], tB[9]
        V(lambda e: e.tensor_tensor(out=abr, in0=magB, in1=cosB, op=ALU.mult), reads=CP, writes=P_)
        V(lambda e: e.tensor_tensor(out=abi, in0=magB, in1=sinB, op=ALU.mult), reads=CP, writes=P_)
        nrB = tB[12]
        V(lambda e: e.tensor_scalar(out=nrB, in0=abr, scalar1=-1.0, scalar2=None, op0=ALU.add), reads=CP, writes=P_)
        V(lambda e: e.tensor_tensor(out=den, in0=lrB, in1=lrB, op=ALU.mult), reads=CP, writes=P_)
        V(lambda e: e.tensor_tensor(out=t1, in0=liB, in1=liB, op=ALU.mult), reads=CP, writes=P_)
        V(lambda e: e.tensor_tensor(out=den, in0=den, in1=t1, op=ALU.add), reads=CP, writes=P_)
        V(lambda e: e.reciprocal(out=den, in_=den), reads=CP, writes=P_)
        V(lambda e: e.tensor_tensor(out=t1, in0=nrB, in1=lrB, op=ALU.mult), reads=CP, writes=P_)
        V(lambda e: e.tensor_tensor(out=t2, in0=abi, in1=liB, op=ALU.mult), reads=CP, writes=P_)
        V(lambda e: e.tensor_tensor(out=t1, in0=t1, in1=t2, op=ALU.add), reads=CP, writes=P_)
        V(lambda e: e.tensor_tensor(out=crB, in0=t1, in1=den, op=ALU.mult), reads=CP, writes=P_)
        V(lambda e: e.tensor_tensor(out=t1, in0=abi, in1=lrB, op=ALU.mult), reads=CP, writes=P_)
        V(lambda e: e.tensor_tensor(out=t2, in0=nrB, in1=liB, op=ALU.mult), reads=CP, writes=P_)
        V(lambda e: e.tensor_tensor(out=t1, in0=t1, in1=t2, op=ALU.subtract), reads=CP, writes=P_)
        V(lambda e: e.tensor_tensor(out=ciB, in0=t1, in1=den, op=ALU.mult), reads=CP, writes=P_)
        bbr = tB[10]
        bbi = tB[11]
        V(lambda e: e.tensor_tensor(out=t1, in0=crB, in1=brB, op=ALU.mult), reads=CP, writes=P_)
        V(lambda e: e.tensor_tensor(out=t2, in0=ciB, in1=biB, op=ALU.mult), reads=CP, writes=P_)
        V(lambda e: e.tensor_tensor(out=bbr, in0=t1, in1=t2, op=ALU.subtract), reads=CP, writes=P_)
        V(lambda e: e.tensor_tensor(out=t1, in0=crB, in1=biB, op=ALU.mult), reads=CP, writes=P_)
        V(lambda e: e.tensor_tensor(out=t2, in0=ciB, in1=brB, op=ALU.mult), reads=CP, writes=P_)
        V(lambda e: e.tensor_tensor(out=bbi, in0=t1, in1=t2, op=ALU.add), reads=CP, writes=P_)
        b0r = tB[13]
        b0i = tB[14]
        V(lambda e: e.tensor_tensor(out=t1, in0=abr, in1=bbr, op=ALU.mult), reads=CP, writes=P_)
        V(lambda e: e.tensor_tensor(out=t2, in0=abi, in1=bbi, op=ALU.mult), reads=CP, writes=P_)
        V(lambda e: e.tensor_tensor(out=b0r, in0=t1, in1=t2, op=ALU.subtract), reads=CP, writes=P_)
        V(lambda e: e.tensor_tensor(out=t1, in0=abr, in1=bbi, op=ALU.mult), reads=CP, writes=P_)
        V(lambda e: e.tensor_tensor(out=t2, in0=abi, in1=bbr, op=ALU.mult), reads=CP, writes=P_)
        V(lambda e: e.tensor_tensor(out=b0i, in0=t1, in1=t2, op=ALU.add), reads=CP, writes=P_)
        for sl, (xr, xi) in enumerate(((b0r, b0i), (bbr, bbi))):
            xr3 = xr.rearrange("p (t q) -> p t q", t=4)
            xi3 = xi.rearrange("p (t q) -> p t q", t=4)
            V(lambda e, sl=sl, xr3=xr3: e.tensor_copy(out=Bvv[:, sl, :, 0, 0:64], in_=xr3), reads=CP, writes=P_)
            V(lambda e, sl=sl, xi3=xi3: e.tensor_copy(out=Bvv[:, sl, :, 0, 64:128], in_=xi3), reads=CP, writes=P_)
            V(lambda e, sl=sl, xi3=xi3: e.tensor_copy(out=Bvv[:, sl, :, 1, 0:64], in_=xi3), reads=CP, writes=P_)
            V(lambda e, sl=sl, xr3=xr3: e.tensor_scalar(out=Bvv[:, sl, :, 1, 64:128], in0=xr3, scalar1=-1.0, scalar2=None, op0=ALU.mult), reads=CP, writes=P_)
            for gq in range(4):
                V(lambda e, gq=gq, sl=sl: e.tensor_scalar(out=Bp[:, sl, :, gq, :, :], in0=Bvv[:, sl], scalar1=mask4[:, gq:gq + 1], scalar2=None, op0=ALU.mult),
                  reads=CP, writes=C_)

        V(lambda e: e.tensor_scalar(out=Cv[:, :, 0, :], in0=pC_t[:, :, 0, :], scalar1=sgnA, scalar2=None, op0=ALU.mult), reads=CP, writes=P_)
        V(lambda e: e.tensor_scalar(out=Cv[:, :, 1, :], in0=pC_t[:, :, 1, :], scalar1=-1.0, scalar2=None, op0=ALU.mult), reads=CP, writes=P_)
        arb = arA.unsqueeze(2).broadcast_to([128, 32, 16])
        aib = aiA.unsqueeze(2).broadcast_to([128, 32, 16])
        ct1 = rcs0[:].rearrange("p (g h) -> p g h", h=16)
        ct2 = tmpE[1][:].rearrange("p (g h) -> p g h", h=16)
        V(lambda e: e.tensor_tensor(out=ct1, in0=Cv[:, :, 0, :], in1=arb, op=ALU.mult), reads=CP, writes=P_)
        V(lambda e: e.tensor_tensor(out=ct2, in0=Cv[:, :, 1, :], in1=aib, op=ALU.mult), reads=CP, writes=P_)
        V(lambda e: e.tensor_tensor(out=Cav[:, :, 0, :], in0=ct1, in1=ct2, op=ALU.add), reads=CP, writes=P_)
        V(lambda e: e.tensor_tensor(out=ct1, in0=Cv[:, :, 1, :], in1=arb, op=ALU.mult), reads=CP, writes=P_)
        V(lambda e: e.tensor_tensor(out=ct2, in0=Cv[:, :, 0, :], in1=aib, op=ALU.mult), reads=CP, writes=P_)
        V(lambda e: e.tensor_tensor(out=Cav[:, :, 1, :], in0=ct1, in1=ct2, op=ALU.subtract), reads=CP, writes=P_)
        V(lambda e: e.memset(Cp[:], 0.0), writes=C_)
        for sl, cvx in enumerate((Cav, Cv)):
            Cp5 = Cp[:, sl].rearrange("p (a b) v c -> p a b v c", b=4)
            Cv5 = cvx.rearrange("p (a b) v h -> p a b v h", b=4)
            for gq in range(4):
                V(lambda e, gq=gq, Cp5=Cp5, Cv5=Cv5: e.tensor_copy(out=Cp5[:, :, gq, :, gq * 16:(gq + 1) * 16], in_=Cv5[:, :, gq, :, :]), reads=CP, writes=C_)
        ynf = yn[:].rearrange("p a b -> p (a b)")
        Bvh = ynf[:, 0:512].rearrange("p (t c) -> p t c", t=4)
        Bvl = ynf[:, 512:1024].rearrange("p (t c) -> p t c", t=4)
        Bsth = ynf[:, 1024:1536].rearrange("p (t c) -> p t c", t=4)
        Bstl = ynf[:, 1536:2048].rearrange("p (t c) -> p t c", t=4)
        Csth = ynf[:, 2048:2560]
        Cstl = ynf[:, 2560:3072]
        dif = rcs[0][:]
        V(lambda e: e.tensor_copy(out=Bvh, in_=Bvv[:, 1, :, 0, :]), reads=CP, writes=P_)
        V(lambda e: e.tensor_tensor(out=dif.rearrange("p (t c) -> p t c", t=4), in0=Bvv[:, 1, :, 0, :], in1=Bvh, op=ALU.subtract), reads=CP, writes=P_)
        V(lambda e: e.tensor_copy(out=Bvl, in_=dif.rearrange("p (t c) -> p t c", t=4)), reads=CP, writes=P_)
        cv0 = Cv[:, :, 0, :]
        V(lambda e: e.tensor_copy(out=Csth.rearrange("p (g h) -> p g h", h=16), in_=cv0), reads=CP, writes=P_)
        V(lambda e: e.tensor_tensor(out=dif.rearrange("p (g h) -> p g h", h=16), in0=cv0, in1=Csth.rearrange("p (g h) -> p g h", h=16), op=ALU.subtract), reads=CP, writes=P_)
        V(lambda e: e.tensor_copy(out=Cstl, in_=dif), reads=CP, writes=P_)
        for t in range(4):
            k.op("pe", lambda e, t=t: e.transpose(PT[0][:, t, :], Bvh[:, t, :], ident_bf[:]), reads=CP, writes=[b_PT[0]], signal=False)
            k.op("pe", lambda e, t=t: e.transpose(PT[0][:, 4 + t, :], Bvl[:, t, :], ident_bf[:]), reads=CP, writes=[b_PT[0]], signal=(t == 3))
        A(lambda e: e.activation(out=Bsth, in_=PT[0][:, 0:4, :], func=AF.Copy), reads=[b_PT[0]], writes=P_)
        A(lambda e: e.activation(out=Bstl, in_=PT[0][:, 4:8, :], func=AF.Copy), reads=[b_PT[0]], writes=P_)
        for t in range(4):
            cs = slice(t * 128, (t + 1) * 128)
            k.op("pe", lambda e, t=t, cs=cs: e.matmul(PS[0][:, cs], lhsT=Bsth[:, t, :], rhs=Csth[:, cs], start=True, stop=False), reads=CP, writes=[b_PS[0]], signal=False)
            k.op("pe", lambda e, t=t, cs=cs: e.matmul(PS[0][:, cs], lhsT=Bsth[:, t, :], rhs=Cstl[:, cs], start=False, stop=False), reads=CP, writes=[b_PS[0]], signal=False)
            k.op("pe", lambda e, t=t, cs=cs: e.matmul(PS[0][:, cs], lhsT=Bstl[:, t, :], rhs=Csth[:, cs], start=False, stop=True), reads=CP, writes=[b_PS[0]], signal=(t == 3))
        for t in range(4):
            cs = slice(t * 128, (t + 1) * 128)
            V(lambda e, t=t: e.tensor_scalar(out=Dm[:, 1, t, :], in0=cst_t[:, 0, :], scalar1=pV_t[:, t:t + 1], scalar2=None, op0=ALU.mult), reads=CP, writes=C_)
            V(lambda e, t=t, cs=cs: e.tensor_tensor(out=rcs1p[:], in0=PS[0][:, cs], in1=cst_t[:, 2, :], op=ALU.mult), reads=[b_PS[0], b_const, b_pro], writes=P_)
            V(lambda e, t=t: e.scalar_tensor_tensor(out=Dm[:, 0, t, :], in0=cst_t[:, 0, :], scalar=pV_t[:, t:t + 1], in1=rcs1p[:], op0=ALU.mult, op1=ALU.add),
              reads=CP, writes=C_)
        for en in ("act", "dve", "pe", "pool"):
            k.wait_all(en, [b_pro, b_const, b_big])

        stg_f = [yg32[:, 0:2, :].rearrange("p a b -> p (a b)"), yg32[:, 2:4, :].rearrange("p a b -> p (a b)")]
        stg_b = [yn[:, 0:2, :].rearrange("p a b -> p (a b)"), yn[:, 2:4, :].rearrange("p a b -> p (a b)")]
        b_stgf = k.bufs("stgf", 2)
        b_stgb = k.bufs("stgb", 2)
        one_col = pA_t[:, 97:98]
        chunks = []
        for kk in range(8):
            for c0 in range(0, 2048, 1024):
                chunks.append((w_in[kk * 128:(kk + 1) * 128, c0:c0 + 1024], pV_t[:, 16 + kk:17 + kk], win_s[:, c0 // 512:c0 // 512 + 2, kk, :], 1024))
        for kk in range(4):
            chunks.append((w_glu[kk * 128:(kk + 1) * 128, :], one_col, wglu_s[:, kk, :], 512))
        for kk in range(8):
            chunks.append((w_out[kk * 128:(kk + 1) * 128, :], pV_t[:, 32 + kk:33 + kk], wout_s[:, :, kk, :], 1024))
        late_chunks = []
        for kk in range(8):
            for c0 in range(0, 4096, 1024):
                late_chunks.append((w_up[kk * 128:(kk + 1) * 128, c0:c0 + 1024], pV_t[:, 24 + kk:25 + kk], wup_s[:, c0 // 512:c0 // 512 + 2, kk, :], 1024))
        for kk in range(32):
            late_chunks.append((w_down[kk * 128:(kk + 1) * 128, :], one_col, wdn_s[:, :, kk, :], 1024))
        NLS = 4
        lstg_f = [bigfflat[:, 1024 * i:1024 * (i + 1)] for i in range(NLS)]
        lstg_b = [bigflat[:, 8192 + 1024 * i: 8192 + 1024 * (i + 1)] for i in range(NLS)]
        b_lstgf = k.bufs("lstgf", NLS)
        b_lstgb = k.bufs("lstgb", NLS)
        late_state = [0]

        def late_load(ci, first=False):
            src, g, dst, cw = late_chunks[ci]
            i = ci % NLS
            k.dma("sp", [(lstg_f[i][:, 0:cw], src)], writes=[b_lstgf[i]] + (MIX_ALL if first else []))

        def late_convert(n):
            for _ in range(n):
                ci = late_state[0]
                if ci >= len(late_chunks):
                    return
                if ci == 0:
                    for c_ in range(NLS - 1):
                        late_load(c_, first=(c_ == 0))
                if ci + NLS - 1 < len(late_chunks):
                    late_load(ci + NLS - 1)
                src, gcol, dst, cw = late_chunks[ci]
                i = ci % NLS
                A(lambda e, i=i, cw=cw, gcol=gcol: e.activation(out=lstg_b[i][:, 0:cw], in_=lstg_f[i][:, 0:cw], func=AF.Copy, scale=gcol),
                  reads=[b_lstgf[i], b_const], writes=[b_lstgb[i]])
                k.dma("sp", [(dst, lstg_b[i][:, 0:cw].rearrange("p (a b) -> p a b", b=512))], reads=[b_lstgb[i]], sembuf=b_lstgb[i])
                late_state[0] += 1

        def cv_load(ci):
            src, g, dst, cw = chunks[ci]
            i = ci % 2
            k.dma("sp", [(stg_f[i][:, 0:cw], src)], writes=[b_stgf[i]])

        cv_load(0)
        for ci in range(len(chunks)):
            src, gcol, dst, cw = chunks[ci]
            i = ci % 2
            if ci + 1 < len(chunks):
                cv_load(ci + 1)
            if i == 0:
                A(lambda e, i=i, cw=cw, gcol=gcol: e.activation(out=stg_b[i][:, 0:cw], in_=stg_f[i][:, 0:cw], func=AF.Copy, scale=gcol),
                  reads=[b_stgf[i], b_const], writes=[b_stgb[i]])
            else:
                V(lambda e, i=i, cw=cw, gcol=gcol: e.tensor_scalar(out=stg_b[i][:, 0:cw], in0=stg_f[i][:, 0:cw], scalar1=gcol, scalar2=None, op0=ALU.mult),
                  reads=[b_stgf[i], b_const], writes=[b_stgb[i]])
            srcv = stg_b[i][:, 0:cw] if len(dst.shape) == 2 else stg_b[i][:, 0:cw].rearrange("p (a b) -> p a b", b=512)
            k.dma("sp", [(dst, srcv)], reads=[b_stgb[i]], sembuf=b_stgb[i])
        k.wait_all("sp", b_stgb + b_stgf + [b_pro, b_const, b_big])

        def combined_plan(with_next):
            plan = []
            if with_next:
                plan.append(("s", 0))
            si = 1
            c2_done = not with_next
            for ui in range(64):
                plan.append(("u", ui))
                if ui == 49:
                    plan.append(("lx",))
                if with_next and ui % S5_EVERY == S5_OFF and si <= 16:
                    plan.append(("s", si))
                    si += 1
                    if si == 17:
                        plan.append(("c2a",))
                elif with_next and si > 16 and not c2_done and ui % 2 == 1 and C2_INSIDE:
                    plan.append(("c2",))
                    c2_done = True
            assert si > 16 or not with_next
            return plan

        pieces = []
        WIN_ORDER = (1, 3, 2, 0)
        for j in WIN_ORDER:
            pieces.append(("win", j))
        pieces.append(("wglu", 0))
        for blk in range(nblk):
            if blk + 1 < nblk:
                for j in WIN_ORDER:
                    pieces.append(("win", j))
            for j in range(2):
                pieces.append(("wout", j))
            for ev in combined_plan(blk + 1 < nblk):
                if ev[0] == "u":
                    ui = ev[1]
                    if ui < 32 and ui % 4 == 0:
                        pieces.append(("wup", ui // 4))
                    elif ui >= 32 and ui % 2 == 0:
                        pieces.append(("wdn", ((ui - 32) // 2) % 8))
                elif ev[0] == "c2a":
                    pieces.append(("wglu", 0))
        issued = [0]

        def piece_src(kind, j):
            if kind == "win":
                return win_s[:, j, :, :], 8
            if kind == "wglu":
                return wglu_s[:, :, :], 4
            if kind == "wout":
                return wout_s[:, j, :, :], 8
            if kind == "wup":
                return wup_s[:, j, :, :], 8
            dh, fq = divmod(j, 4)
            return wdn_s[:, dh, fq * 8:(fq + 1) * 8, :], 8

        def prefetch(upto):
            while issued[0] <= min(upto, len(pieces) - 1):
                i = issued[0]
                kind, j = pieces[i]
                src, nk = piece_src(kind, j)
                slot = i % NR
                h_ = nk // 2
                k.dma("sp", [(ring[slot][:, 0:h_, :], src[:, 0:h_, :]), (ring[slot][:, h_:nk, :], src[:, h_:nk, :])], writes=[b_ring[slot]])
                issued[0] += 1

        pidx = [0]

        def next_piece(kind, j, hold=0):
            i = pidx[0]
            assert pieces[i] == (kind, j), (pieces[i], kind, j)
            prefetch(i + NR - 1 - hold)
            pidx[0] += 1
            return ring[i % NR], b_ring[i % NR]

        def load_x(blk, tiles=(0, 1, 2, 3)):
            par = blk % 2
            for j in tiles:
                r0 = blk * NB + j * 128
                k.dma("sp", [(xs[par][:, j, :], x[r0:r0 + 128, :])], writes=[b_xs[par][j]])

        def rstd_from_ssq(ssq_ap, out_ap, n, bst):
            A(lambda e: e.activation(out=out_ap, in_=ssq_ap, func=AF.Sqrt, scale=1.0 / n, bias=EPS), reads=[bst], writes=[bst])
            V(lambda e: e.reciprocal(out=out_ap, in_=out_ap), reads=[bst], writes=[bst])

        def tok_prep_a(par, j, scol, bst):
            xt = xs[par][:, j, :]
            bx = b_xs[par][j]
            A(lambda e: e.activation(out=junk[:], in_=xt, func=AF.Square, accum_out=stat[:, scol:scol + 1]),
              reads=[bx], writes=[bst, b_sq[0], b_sq[1]])
            rstd_from_ssq(stat[:, scol:scol + 1], stat[:, scol + 1:scol + 2], float(D), bst)
            V(lambda e: e.tensor_scalar(out=xbn[j % 2][:], in0=xt, scalar1=stat[:, scol + 1:scol + 2], scalar2=None, op0=ALU.mult),
              reads=[bx, bst], writes=[b_xbn[j % 2]])

        def tok_prep_b(j, pt=0):
            for kk in range(8):
                k.op("pe", lambda e, kk=kk: e.transpose(PT[pt][:, kk, :], xbn[j % 2][:, kk * 128:(kk + 1) * 128], ident_bf[:]),
                     reads=[b_xbn[j % 2], b_const], writes=[b_PT[pt]], signal=(kk == 7))
            V(lambda e: e.tensor_copy(out=xT[:, :, j * 128:(j + 1) * 128], in_=PT[pt][:]),
              reads=[b_PT[pt]], writes=[b_xT])

        def proj_fm(wt, bw, m, pi):
            for kk in range(8):
                k.op("pe", lambda e, kk=kk: e.matmul(PS[pi][:], lhsT=wt[:, kk, m * 128:(m + 1) * 128], rhs=xT[:, kk, :],
                                                    start=(kk == 0), stop=(kk == 7)),
                     reads=[bw, b_xT], writes=[b_PS[pi]], signal=(kk == 7))

        psrot = [0]

        def nps():
            i = psrot[0] % 7
            psrot[0] += 1
            return i

        MIX = b_th + b_z + b_bg + [b_tcv]

        def tok_steps(par_, col0, bsts, pt=0):
            a = lambda j: (lambda: tok_prep_a(par_, j, col0 + 2 * j, bsts[j]))
            b = lambda j: (lambda: tok_prep_b(j, pt))
            return [[a(0), a(1)], [b(0), a(2)], [b(1), a(3)], [b(2)], [b(3)]]

        def stage_A(blk, pt=0):
            for grp in tok_steps(blk % 2, 0, b_stat[0:4], pt):
                for f_ in grp:
                    f_()

        def stage_B0(blk):
            wt, bw = next_piece("win", 1)
            for m in range(4):
                pi = nps()
                proj_fm(wt, bw, m, pi)
                A(lambda e, m=m, pi=pi: e.activation(out=th[:, m, :], in_=PS[pi][:], func=AF.Copy),
                  reads=[b_PS[pi]], writes=[b_th[m]] + ((b_h1 + b_lstgf + b_lstgb) if m == 0 else []))
            wt, bw = next_piece("win", 3)
            for m in range(4):
                pi = nps()
                proj_fm(wt, bw, m, pi)
                V(lambda e, m=m, pi=pi: e.tensor_tensor(out=zb[:, m, 2:514], in0=PS[pi][:], in1=th[:, m, :], op=ALU.mult),
                  reads=[b_PS[pi], b_th[m]], writes=[b_z[m]])
                G(lambda e, m=m: e.tensor_copy(out=zb[:, m, 0:2], in_=zhalo[:, m, :]), reads=[b_zh], writes=[b_z[m]])
                cw = lambda jj, m=m: pV_t[:, 4 + m * 3 + jj: 5 + m * 3 + jj]
                V(lambda e, m=m, cw=cw: e.tensor_scalar(out=th[:, m, :], in0=zb[:, m, 0:512], scalar1=cw(0), scalar2=None, op0=ALU.mult),
                  reads=[b_z[m], b_const], writes=[b_th[m]])
                V(lambda e, m=m, cw=cw: e.scalar_tensor_tensor(out=th[:, m, :], in0=zb[:, m, 1:513], scalar=cw(1), in1=th[:, m, :], op0=ALU.mult, op1=ALU.add),
                  reads=[b_z[m], b_const], writes=[b_th[m]])
                V(lambda e, m=m, cw=cw: e.scalar_tensor_tensor(out=th[:, m, :], in0=zb[:, m, 2:514], scalar=cw(2), in1=th[:, m, :], op0=ALU.mult, op1=ALU.add),
                  reads=[b_z[m], b_const], writes=[b_th[m]])
                G(lambda e, m=m: e.tensor_copy(out=zhalo[:, m, :], in_=zb[:, m, 512:514]), reads=[b_z[m]], writes=[b_zh])

        def stage_B1(blk):
            wt, bw = next_piece("win", 2)
            for m in range(4):
                pi = nps()
                proj_fm(wt, bw, m, pi)
                A(lambda e, m=m, pi=pi: e.activation(out=bg[:, m, :], in_=PS[pi][:], func=AF.Copy),
                  reads=[b_PS[pi]], writes=[b_bg[m]])
                V(lambda e, m=m: e.tensor_tensor(out=yc[:, m, :], in0=th[:, m, :], in1=bg[:, m, :], op=ALU.mult),
                  reads=[b_bg[m]], writes=[b_yc[m]])
            wt, bw = next_piece("win", 0)
            for m in range(4):
                pi = nps()
                proj_fm(wt, bw, m, pi)
                A(lambda e, m=m, pi=pi: e.activation(out=ub[:, m, :], in_=PS[pi][:], func=AF.Copy),
                  reads=[b_PS[pi]], writes=[b_ub])

        def stage_B2(blk):
            pass

        def stage_B3(blk):
            pn = nps()
            for m in range(4):
                A(lambda e, m=m: e.activation(out=sq[m % 2], in_=yc[:, m, :], func=AF.Square), reads=[b_yc[m]], writes=[b_sq[m % 2]])
                k.op("pe", lambda e, m=m: e.matmul(PS[pn][:], lhsT=ones_bf[:], rhs=sq[m % 2], start=(m == 0), stop=(m == 3)),
                     reads=[b_sq[m % 2], b_const], writes=[b_PS[pn]], signal=True)
            A(lambda e: e.activation(out=rcs[0][:], in_=PS[pn][:], func=AF.Sqrt, scale=1.0 / 512, bias=EPS), reads=[b_PS[pn]], writes=[b_rcs[0]])
            V(lambda e: e.reciprocal(out=rcs[0][:], in_=rcs[0][:]), reads=[b_rcs[0]], writes=[b_rcs[0]])
            for m in range(4):
                G(lambda e, m=m: e.tensor_tensor(out=yn[:, 4 + m, :], in0=yc[:, m, :], in1=rcs[0][:], op=ALU.mult),
                  reads=[b_yc[m], b_rcs[0]], writes=[b_yn[4 + m]])

        P1v = PS[5][:].rearrange("p (a b) -> p a b", a=4)
        P2v = PS[6][:].rearrange("p (a b) -> p a b", a=4)
        pysl = [PS[7][:, 0:128], PS[7][:, 128:256]]
        pcv = PS[7][:, 256:264]

        def s5_B(s_):
            f, hb = divmod(s_, 8)
            t, hf = divmod(hb, 2)
            rows = slice(64 * hf, 64 * hf + 64)
            for gq in range(4):
                for var, (Pv, bi) in enumerate(((P1v, 5), (P2v, 6))):
                    for sl in range(2):
                        tok = slice(f * 256 + sl, (f + 1) * 256, 2)
                        k.op("pe", lambda e, gq=gq, var=var, Pv=Pv, sl=sl, tok=tok: e.matmul(
                            Pv[:, gq, :], lhsT=Bp[rows, sl, t, gq, var, :], rhs=ub[rows, t, tok], start=(sl == 0), stop=(sl == 1)),
                             reads=[b_ub, b_const], writes=[b_PS[bi]], signal=(gq == 3 and var == 1 and sl == 1))

        def s5_mod(s_):
            f, hb = divmod(s_, 8)
            pr = hb % 2
            g0 = 4 * hb
            V(lambda e: e.tensor_tensor(out=tmpm[pr], in0=P2v, in1=Es[:, g0:g0 + 4, :], op=ALU.mult),
              reads=[b_PS[6], b_const], writes=[b_tmpm[pr]])
            V(lambda e: e.tensor_tensor(out=Vt[pr], in0=P1v, in1=Ec[:, g0:g0 + 4, :], op=ALU.mult),
              reads=[b_PS[5], b_const], writes=[b_Vt[pr]])

        def s5_scan(s_):
            f, hb = divmod(s_, 8)
            t, hf = divmod(hb, 2)
            pr = hb % 2
            g0 = 4 * hb
            V(lambda e: e.tensor_tensor(out=Vt[pr], in0=Vt[pr], in1=tmpm[pr], op=ALU.add),
              reads=[b_Vt[pr], b_tmpm[pr]], writes=[b_Vt[pr]])
            for gq in range(4):
                g = g0 + gq
                V(lambda e, gq=gq, g=g: e.tensor_tensor_scan(out=Wt[pr][:, gq, :], data0=rho[:, g:g + 1].broadcast_to([128, 128]),
                                                         data1=Vt[pr][:, gq, :], initial=cb[:, g:g + 1], op0=ALU.mult, op1=ALU.add),
                  reads=[b_Vt[pr], b_cb[t], b_const], writes=[b_Wt[pr]])
            G(lambda e: e.tensor_copy(out=Q1b[pr][:, :, 0], in_=cb[:, g0:g0 + 4]), reads=[b_cb[t]], writes=[b_Q[pr]])
            G(lambda e: e.tensor_tensor(out=Q1b[pr][:, :, 1:129], in0=Wt[pr], in1=Ec[:, g0:g0 + 4, :], op=ALU.mult),
              reads=[b_Wt[pr], b_const], writes=[b_Q[pr]])
            G(lambda e: e.tensor_tensor(out=Q2b[pr][:, :, 1:129], in0=Wt[pr], in1=Es[:, g0:g0 + 4, :], op=ALU.mult),
              reads=[b_Wt[pr], b_const], writes=[b_Q[pr]])
            G(lambda e: e.tensor_copy(out=wl[:, g0:g0 + 4], in_=Wt[pr][:, :, 127]), reads=[b_Wt[pr]], writes=[b_wl[t]])
            if hf == 1:
                gs = slice(8 * t, 8 * t + 8)
                G(lambda e: e.tensor_tensor(out=cq[:, 0, :], in0=wl[:, gs], in1=Ec[:, gs, 127], op=ALU.mult), reads=[b_wl[t], b_const], writes=[b_cq])
                G(lambda e: e.tensor_tensor(out=cq[:, 1, :], in0=wl[:, gs], in1=Es[:, gs, 127], op=ALU.mult), reads=[b_wl[t], b_const], writes=[b_cq])
                G(lambda e: e.tensor_copy(out=cqb[:, 0:2, :], in_=cq[:, 0:2, :]), reads=[b_cq], writes=[b_cqb])
                G(lambda e: e.tensor_tensor(out=cq[:, 2:4, :], in0=cq[:, 0:2, :], in1=cqb[:, 0:2, :], op=ALU.subtract), reads=[b_cq, b_cqb], writes=[b_cq])
                G(lambda e: e.tensor_copy(out=cqb[:, 2:4, :], in_=cq[:, 2:4, :]), reads=[b_cq], writes=[b_cqb])

        def s5_C(s_):
            f, hb = divmod(s_, 8)
            t, hf = divmod(hb, 2)
            pr = hb % 2
            g0 = 4 * hb
            rows = slice(64 * hf, 64 * hf + 64)
            for sl in range(2):
                qs = slice(0, 128) if sl == 0 else slice(1, 129)
                tok = slice(f * 256 + sl, (f + 1) * 256, 2)
                py = pysl[sl]
                for gq in range(4):
                    g = g0 + gq
                    k.op("pe", lambda e, gq=gq, g=g, sl=sl, qs=qs, py=py: e.matmul(py[rows, :], lhsT=Cp[:, sl, g, 0, :], rhs=Q1b[pr][:, gq, qs], start=(gq == 0), stop=False),
                         reads=[b_Q[pr], b_const], writes=[b_PS[7]], signal=False)
                    k.op("pe", lambda e, gq=gq, g=g, sl=sl, qs=qs, py=py: e.matmul(py[rows, :], lhsT=Cp[:, sl, g, 1, :], rhs=Q2b[pr][:, gq, qs], start=False, stop=False),
                         reads=[b_Q[pr], b_const], writes=[b_PS[7]], signal=False)
                k.op("pe", lambda e, sl=sl, tok=tok, py=py: e.matmul(py[rows, :], lhsT=Dm[rows, sl, t, 64 * hf:64 * hf + 64], rhs=ub[rows, t, tok], start=False, stop=True),
                     reads=[b_ub, b_const], writes=[b_PS[7]], signal=True)
            if hf == 1:
                gs = slice(8 * t, 8 * t + 8)
                k.op("pe", lambda e: e.matmul(pcv, lhsT=ident_bf[:], rhs=cqb[:, 0, :], start=True, stop=False), reads=[b_cqb, b_const], writes=[b_PS[7]], signal=False)
                k.op("pe", lambda e: e.matmul(pcv, lhsT=ident_bf[:], rhs=cqb[:, 2, :], start=False, stop=False), reads=[b_cqb, b_const], writes=[b_PS[7]], signal=False)
                k.op("pe", lambda e: e.matmul(pcv, lhsT=swap_bf[:], rhs=cqb[:, 1, :], start=False, stop=False), reads=[b_cqb, b_const], writes=[b_PS[7]], signal=False)
                k.op("pe", lambda e: e.matmul(pcv, lhsT=swap_bf[:], rhs=cqb[:, 3, :], start=False, stop=True), reads=[b_cqb, b_const], writes=[b_PS[7]], signal=True)
                A(lambda e: e.activation(out=cb[:, gs], in_=pcv, func=AF.Copy), reads=[b_PS[7]], writes=[b_cb[t]])
                for sl in range(2):
                    tok = slice(f * 256 + sl, (f + 1) * 256, 2)
                    A(lambda e, sl=sl, tok=tok: e.activation(out=yg32[:, t, tok], in_=pysl[sl], func=AF.Gelu_apprx_tanh), reads=[b_PS[7]], writes=[b_yg[t]])
                    A(lambda e, sl=sl, tok=tok: e.activation(out=ygb[:, t, tok], in_=pysl[sl], func=AF.Gelu_apprx_tanh), reads=[b_PS[7]], writes=[b_ygb] + b_xbn)

        NS = 16

        def s5_thunks():
            th_ = [lambda: s5_B(0)]
            for s_ in range(NS):
                def step(s_=s_):
                    s5_mod(s_)
                    if s_ + 1 < NS:
                        s5_B(s_ + 1)
                    s5_scan(s_)
                    if s_ >= 1:
                        s5_C(s_ - 1)
                    if s_ == NS - 1:
                        s5_C(NS - 1)
                th_.append(step)
            return th_

        GLU_BANKS = (4, 5, 6, 7)

        def stage_C2a(blk):
            wg, bwg = next_piece("wglu", 0)
            for jt in range(4):
                pi = GLU_BANKS[jt]
                for ft in range(4):
                    k.op("pe", lambda e, ft=ft, jt=jt, pi=pi: e.matmul(PS[pi][:], lhsT=wg[:, ft, jt * 128:(jt + 1) * 128], rhs=ygb[:, ft, :],
                                                                      start=(ft == 0), stop=(ft == 3)),
                         reads=[bwg, b_ygb] + b_xbn, writes=[b_PS[pi]], signal=(ft == 3))

        def stage_C2b(blk):
            pn = GLU_BANKS[0]
            for jt in range(4):
                pi = GLU_BANKS[jt]
                A(lambda e, jt=jt, pi=pi: e.activation(out=sg[jt % 2][:], in_=PS[pi][:], func=AF.Sigmoid), reads=[b_PS[pi]], writes=[b_sg[jt % 2]])
                V(lambda e, jt=jt: e.tensor_tensor(out=yg32[:, jt, :], in0=yg32[:, jt, :], in1=sg[jt % 2][:], op=ALU.mult),
                  reads=[b_yg[jt], b_sg[jt % 2]], writes=[b_yg[jt]])
                A(lambda e, jt=jt: e.activation(out=sq[jt % 2], in_=yg32[:, jt, :], func=AF.Square), reads=[b_yg[jt]], writes=[b_sq[jt % 2]])
                k.op("pe", lambda e, jt=jt: e.matmul(PS[pn][:], lhsT=ones_bf[:], rhs=sq[jt % 2], start=(jt == 0), stop=(jt == 3)),
                     reads=[b_sq[jt % 2], b_const], writes=[b_PS[pn]], signal=True)
            A(lambda e: e.activation(out=rcs[1][:], in_=PS[pn][:], func=AF.Sqrt, scale=1.0 / 512, bias=EPS), reads=[b_PS[pn]], writes=[b_rcs[1]])
            V(lambda e: e.reciprocal(out=rcs[1][:], in_=rcs[1][:]), reads=[b_rcs[1]], writes=[b_rcs[1]])
            for jt in range(4):
                V(lambda e, jt=jt: e.tensor_tensor(out=yn[:, jt, :], in0=yg32[:, jt, :], in1=rcs[1][:], op=ALU.mult),
                  reads=[b_yg[jt], b_rcs[1]], writes=[b_yn[jt]])

        def stage_D(blk):
            par = blk % 2
            wo0, bwo0 = next_piece("wout", 0)
            wo1, bwo1 = next_piece("wout", 1, hold=1)
            for j in range(4):
                js = slice(j * 128, (j + 1) * 128)
                pis = []
                for dh, (wo, bwo) in enumerate(((wo0, bwo0), (wo1, bwo1))):
                    pi = nps()
                    pis.append(pi)
                    for ft in range(8):
                        k.op("pe", lambda e, ft=ft, pi=pi, wo=wo: e.matmul(PS[pi][:], lhsT=yn[:, ft, js], rhs=wo[:, ft, :], start=(ft == 0), stop=(ft == 7)),
                             reads=[bwo, b_yn[ft]], writes=[b_PS[pi]], signal=(ft == 7))
                    A(lambda e, pi=pi, dh=dh, j=j: e.activation(out=junk[:, 0:512], in_=PS[pi][:], func=AF.Square, accum_out=stat[:, 32 + 4 * j + dh:33 + 4 * j + dh]),
                      reads=[b_PS[pi]], writes=[b_statD[j], b_sq[0]])
                c0 = 32 + 4 * j
                V(lambda e, c0=c0: e.tensor_tensor(out=stat[:, c0 + 2:c0 + 3], in0=stat[:, c0:c0 + 1], in1=stat[:, c0 + 1:c0 + 2], op=ALU.add), reads=[b_statD[j]], writes=[b_statD[j]])
                rstd_from_ssq(stat[:, c0 + 2:c0 + 3], stat[:, c0 + 3:c0 + 4], float(D), b_statD[j])
                for dh in range(2):
                    pi = pis[dh]
                    ds = slice(dh * 512, (dh + 1) * 512)
                    V(lambda e, pi=pi, ds=ds, dh=dh: e.tensor_tensor(out=tmpE[dh][:], in0=PS[pi][:], in1=gpm_bc[:, ds], op=ALU.mult),
                      reads=[b_PS[pi], b_const], writes=[b_tmpE[dh]])
                    V(lambda e, ds=ds, dh=dh, c0=c0, j=j: e.scalar_tensor_tensor(out=xs[par][:, j, ds], in0=tmpE[dh][:], scalar=stat[:, c0 + 3:c0 + 4], in1=xs[par][:, j, ds],
                                                                   op0=ALU.mult, op1=ALU.add),
                      reads=[b_tmpE[dh], b_statD[j], b_xs[par][j]], writes=[b_xs[par][j]])

        def stage_E0(blk):
            for grp in tok_steps(blk % 2, 8, b_stat[4:8]):
                for f_ in grp:
                    f_()

        def mlp_units(blk):
            par = blk % 2
            units = []
            st_ = {}

            def up(q, tt):
                if tt == 0:
                    st_["wu"] = next_piece("wup", q)
                wu, bwu = st_["wu"]
                ff = q * 4 + tt
                pi = ff % 5
                proj_fm(wu, bwu, tt, pi)
                A(lambda e: e.activation(out=rl[ff % 2][:], in_=PS[pi][:], func=AF.Relu), reads=[b_PS[pi]], writes=[b_rl[ff % 2]])
                G(lambda e: e.tensor_tensor(out=h1T[:, ff, :], in0=rl[ff % 2][:], in1=rl[ff % 2][:], op=ALU.mult),
                  reads=[b_rl[ff % 2]], writes=[b_h1[ff]] + ((MIX + b_lstgf + b_lstgb) if ff == 0 else []))

            def down(p_, dh, fq, jj):
                j = 2 * p_ + jj
                bank = 2 * jj + dh
                if jj == 0:
                    st_["wd"] = next_piece("wdn", dh * 4 + fq)
                wd, bwd = st_["wd"]
                js = slice(j * 128, (j + 1) * 128)
                for kk in range(8):
                    ff = fq * 8 + kk
                    k.op("pe", lambda e, kk=kk, ff=ff: e.matmul(
                        PS[bank][:], lhsT=h1T[:, ff, js], rhs=wd[:, kk, :], start=(fq == 0 and kk == 0), stop=(fq == 3 and kk == 7)),
                         reads=[bwd, b_h1[ff]], writes=[b_PS[bank]], signal=(kk == 7))
                if fq == 3:
                    c0 = 48 + 4 * j
                    A(lambda e: e.activation(out=junk[:, 0:512], in_=PS[bank][:], func=AF.Square, accum_out=stat[:, c0 + dh:c0 + dh + 1]),
                      reads=[b_PS[bank]], writes=[b_statE[j], b_sq[0]])
                    if dh == 1:
                        V(lambda e: e.tensor_tensor(out=stat[:, c0 + 2:c0 + 3], in0=stat[:, c0:c0 + 1], in1=stat[:, c0 + 1:c0 + 2], op=ALU.add),
                          reads=[b_statE[j]], writes=[b_statE[j]])
                        rstd_from_ssq(stat[:, c0 + 2:c0 + 3], stat[:, c0 + 3:c0 + 4], float(D), b_statE[j])
                        for hh in range(2):
                            bk = 2 * jj + hh
                            ds = slice(hh * 512, (hh + 1) * 512)
                            V(lambda e, bk=bk, ds=ds, hh=hh: e.tensor_tensor(out=tmpE[hh][:], in0=PS[bk][:], in1=gpo_bc[:, ds], op=ALU.mult),
                              reads=[b_PS[bk], b_const], writes=[b_tmpE[hh]])
                            V(lambda e, ds=ds, hh=hh: e.scalar_tensor_tensor(out=xs[par][:, j, ds], in0=tmpE[hh][:], scalar=stat[:, c0 + 3:c0 + 4], in1=xs[par][:, j, ds],
                                                                      op0=ALU.mult, op1=ALU.add),
                              reads=[b_tmpE[hh], b_statE[j], b_xs[par][j]], writes=[b_xs[par][j]])
                        r0 = blk * NB + j * 128
                        k.dma("sp", [(out[r0:r0 + 128, :], xs[par][:, j, :])], reads=[b_xs[par][j]], sembuf=b_xs[par][j])

            for q in range(8):
                for tt in range(4):
                    units.append(lambda q=q, tt=tt: up(q, tt))
            for p_ in range(2):
                for dh in range(2):
                    for fq in range(4):
                        for jj in range(2):
                            units.append(lambda p_=p_, dh=dh, fq=fq, jj=jj: down(p_, dh, fq, jj))
            return units

        load_x(0)
        if nblk > 1:
            load_x(1)
        stage_A(0)
        stage_B0(0)
        stage_B1(0)
        stage_B2(0)
        stage_B3(0)
        for f_ in s5_thunks():
            f_()
            late_convert(4)
        late_convert(len(late_chunks))
        k.wait_all("sp", b_lstgb + b_lstgf)
        stage_C2a(0)
        for blk in range(nblk):
            has_next = blk + 1 < nblk
            if has_next:
                stage_A(blk + 1, pt=1)
            stage_C2b(blk)
            if has_next:
                stage_B0(blk + 1)
                stage_B1(blk + 1)
                stage_B2(blk + 1)
            stage_D(blk)
            stage_E0(blk)
            if has_next:
                stage_B3(blk + 1)
            units = mlp_units(blk)
            s5 = s5_thunks() if has_next else None
            for ev in combined_plan(has_next):
                if ev[0] == "u":
                    units[ev[1]]()
                elif ev[0] == "s":
                    s5[ev[1]]()
                elif ev[0] == "c2a":
                    stage_C2a(blk + 1)
                elif ev[0] == "lx":
                    if blk + 2 < nblk:
                        load_x(blk + 2, (0, 1))
            if blk + 2 < nblk:
                load_x(blk + 2, (2, 3))

        if _STOP:
            for en in ("act", "dve", "pe", "pool"):
                k.wait_all("sp", [b_PS[i] for i in range(7)] + b_yn + b_yg + b_Q + b_cb + [b_xT, b_ub])
        k.wait_all("sp", b_xs[0] + b_xs[1])
        _build.stats = dict(k.ninstr)
    return nc


def _host_params(inp):
    f = np.float32
    lam_re = np.asarray(inp["lam_re"], f)[0]
    lam_im = np.asarray(inp["lam_im"], f)[0]
    log_dt = np.asarray(inp["log_dt"], f)[0]
    b_re = np.asarray(inp["b_re"], f)[0]
    b_im = np.asarray(inp["b_im"], f)[0]
    c_re = np.asarray(inp["c_re"], f)[0]
    c_im = np.asarray(inp["c_im"], f)[0]
    d_skip = np.asarray(inp["d_skip"], f)[0]
    conv_w = np.asarray(inp["conv_w"], f)[0]

    pA = np.zeros((128, 225), f)
    pA[:, 0:32] = np.concatenate([lam_re.T, lam_re.T], axis=0)
    pA[:, 32:64] = np.concatenate([lam_im.T, lam_im.T], axis=0)
    pA[:, 64:96] = np.broadcast_to(log_dt[None, :], (128, 32))
    pA[0:64, 96] = 1.0
    pA[64:128, 96] = -1.0
    pA[:, 97:225] = np.broadcast_to(np.arange(1, 129, dtype=f)[None, :], (128, 128))

    def orientB(a_gp):
        a = a_gp.reshape(4, 8, 64)
        a = np.broadcast_to(a[:, :, None, :], (4, 8, 16, 64))
        return np.ascontiguousarray(a.transpose(1, 2, 0, 3)).reshape(128, 4 * 64)

    def orientB_b(b_gph):
        a = b_gph.reshape(4, 8, 64, 16)
        return np.ascontiguousarray(a.transpose(1, 3, 0, 2)).reshape(128, 4 * 64)

    pB = np.zeros((128, 5, 256), f)
    pB[:, 0] = orientB(lam_re)
    pB[:, 1] = orientB(lam_im)
    pB[:, 2] = orientB(np.broadcast_to(log_dt[:, None], (32, 64)))
    pB[:, 3] = orientB_b(b_re)
    pB[:, 4] = orientB_b(b_im)

    crT = np.ascontiguousarray(c_re.transpose(2, 0, 1))
    ciT = np.ascontiguousarray(c_im.transpose(2, 0, 1))
    pC = np.zeros((128, 32, 2, 16), f)
    pC[0:64, :, 0, :] = crT
    pC[64:128, :, 0, :] = ciT
    pC[0:64, :, 1, :] = ciT
    pC[64:128, :, 1, :] = crT

    pV = np.zeros((128, 64), f)
    pV[:, 0:4] = d_skip.reshape(4, 8 * 16).T
    for t in range(4):
        for j in range(3):
            pV[:, 4 + t * 3 + j] = conv_w[j, t * 128:(t + 1) * 128]
    pV[:, 16:24] = np.asarray(inp["g_pre_mix"], f)[0].reshape(8, 128).T
    pV[:, 24:32] = np.asarray(inp["g_pre_mlp"], f)[0].reshape(8, 128).T
    gcat = np.concatenate([np.asarray(inp["g_ssm_out"], f)[0], np.asarray(inp["g_conv_out"], f)[0]])
    pV[:, 32:40] = gcat.reshape(8, 128).T
    r = np.arange(128)
    for gq in range(4):
        pV[:, 40 + gq] = ((r // 16) % 4 == gq).astype(f)

    gvec = np.stack([np.asarray(inp["g_post_mix"], f)[0], np.asarray(inp["g_post_mlp"], f)[0]], axis=0)

    cst = np.zeros((128, 3, 128), f)
    cst[:, 0, :] = np.eye(128, dtype=f)
    rr = np.arange(128)
    cst[:, 2, :] = (rr[:, None] // 16 == rr[None, :] // 16).astype(f)
    for p in range(64):
        cst[64 + p, 1, p] = -1.0
        cst[p, 1, 64 + p] = 1.0
    return dict(pA=pA, pB=pB, pC=pC, pV=pV, gvec=np.ascontiguousarray(gvec), cst=cst)


def _shared_inputs(inp):
    f = np.float32
    d = _host_params(inp)
    d["w_in"] = np.ascontiguousarray(np.asarray(inp["w_in"], f)[0])
    d["w_glu"] = np.ascontiguousarray(np.asarray(inp["w_glu"], f)[0])
    d["w_out"] = np.ascontiguousarray(np.asarray(inp["w_out"], f)[0])
    d["w_up"] = np.ascontiguousarray(np.asarray(inp["w_up"], f)[0])
    d["w_down"] = np.ascontiguousarray(np.asarray(inp["w_down"], f)[0])
    return d


_NC_CACHE = {}


def kernel(**inputs):
    x = np.asarray(inputs["x"], np.float32)
    nb, L, _ = x.shape
    nblk = L // NB
    key = nblk
    if key not in _NC_CACHE:
        _NC_CACHE[key] = _build(nblk)
    nc = _NC_CACHE[key]
    shared = _shared_inputs(inputs)
    in_maps = []
    for c in range(nb):
        m = dict(shared)
        m["x"] = np.ascontiguousarray(x[c])
        in_maps.append(m)
    res = run_bass_kernel_spmd(nc, in_maps, core_ids=list(range(nb)))
    return np.stack([np.asarray(r["out"], np.float32) for r in res.results], axis=0)
```
